# Optimizing a Trainium2 kernel written in Bass

```python
import math
import jax, jax.numpy as jnp
from jax import lax
import numpy as np

D_MODEL = 1024
BATCH = 8
SEQ = 4096
DEPTH = 2

ATTN_HEADS = 8
ATTN_HEAD_DIM = 64
ATTN_WIDTH = ATTN_HEADS * ATTN_HEAD_DIM
HGRN_HEADS = 4
HGRN_HEAD_DIM = 128
HGRN_WIDTH = HGRN_HEADS * HGRN_HEAD_DIM
MIX_WIDTH = ATTN_WIDTH + HGRN_WIDTH
IN_PROJ_WIDTH = 3 * ATTN_WIDTH + 4 * HGRN_WIDTH
DILATED_PATTERNS = ((128, 1), (512, 4), (2048, 16))
ROPE_THETA = 10000.0
HGRN_CHUNK = 16
MLP_HIDDEN = 4 * D_MODEL
NORM_EPS = 1e-6
MASK_VALUE = -1e30

kernel_name = 'hymba_hgrn2_dilated_swa_hybrid'


def rms_norm(x, gain):
    xf = x.astype(jnp.float32)
    xf = xf * lax.rsqrt(jnp.mean(xf * xf, axis=-1, keepdims=True) + NORM_EPS)
    return (xf * gain.astype(jnp.float32)).astype(x.dtype)


def split_heads(a, n_heads, head_dim):
    b, s, _ = a.shape
    return a.reshape(b, s, n_heads, head_dim).transpose(0, 2, 1, 3)


def merge_heads(a):
    b, h, s, d = a.shape
    return a.transpose(0, 2, 1, 3).reshape(b, s, h * d)


def rotary(x, positions):
    half = x.shape[-1] // 2
    inv_freq = ROPE_THETA ** (-jnp.arange(half, dtype=jnp.float32) / half)
    ang = positions.astype(jnp.float32)[:, None] * inv_freq[None, :]
    cos, sin = jnp.cos(ang), jnp.sin(ang)
    x1, x2 = x[..., :half], x[..., half:]
    return jnp.concatenate([x1 * cos - x2 * sin, x1 * sin + x2 * cos], axis=-1)


def dilated_window_attention(q, k, v, window, dilation):
    b, h, s, d = q.shape
    span = window // dilation
    unit = dilation * span
    s_pad = -(-s // unit) * unit
    pad = ((0, 0), (0, 0), (0, s_pad - s), (0, 0))
    q, k, v = jnp.pad(q, pad), jnp.pad(k, pad), jnp.pad(v, pad)
    n_sub = s_pad // dilation
    n_blk = n_sub // span

    def to_sub(a):
        a = a.reshape(b, h, n_sub, dilation, d).transpose(0, 1, 3, 2, 4)
        return a.reshape(b, h, dilation, n_blk, span, d)

    def with_prev(a):
        prev = jnp.pad(a, ((0, 0), (0, 0), (0, 0), (1, 0), (0, 0), (0, 0)))[:, :, :, :-1]
        return jnp.concatenate([prev, a], axis=4)

    qs = to_sub(q)
    kc, vc = with_prev(to_sub(k)), with_prev(to_sub(v))
    scores = jnp.einsum('bhrnid,bhrnjd->bhrnij', qs, kc)
    i = jnp.arange(span)[:, None]
    j = jnp.arange(2 * span)[None, :]
    dist = span + i - j
    band = (dist >= 0) & (dist <= span)
    blk = jnp.arange(n_blk)[:, None, None]
    valid = band[None] & ((blk > 0) | (j >= span)[None])
    scores = jnp.where(valid, scores, MASK_VALUE)
    m = jnp.max(scores, axis=-1, keepdims=True)
    p = jnp.where(valid, jnp.exp(scores - m), 0.0)
    l = jnp.sum(p, axis=-1, keepdims=True)
    o = jnp.einsum('bhrnij,bhrnjd->bhrnid', p, vc) / l
    lse = (m + jnp.log(l))[..., 0]

    def from_sub(a):
        tail = a.shape[5:]
        a = a.reshape((b, h, dilation, n_sub) + tail)
        a = jnp.moveaxis(a, 2, 3)
        return a.reshape((b, h, s_pad) + tail)[:, :, :s]

    return from_sub(o), from_sub(lse)


def dilated_attention_group(q_a, k_a, v_a, positions):
    q = rotary(split_heads(q_a, ATTN_HEADS, ATTN_HEAD_DIM).astype(jnp.float32), positions)
    q = q * (ATTN_HEAD_DIM ** -0.5)
    k = rotary(split_heads(k_a, ATTN_HEADS, ATTN_HEAD_DIM).astype(jnp.float32), positions)
    v = split_heads(v_a, ATTN_HEADS, ATTN_HEAD_DIM).astype(jnp.float32)
    outs, lses = [], []
    for window, dilation in DILATED_PATTERNS:
        o, lse = dilated_window_attention(q, k, v, window, dilation)
        outs.append(o)
        lses.append(lse)
    weights = jax.nn.softmax(jnp.stack(lses, axis=0), axis=0)
    o = jnp.einsum('pbhs,pbhsd->bhsd', weights, jnp.stack(outs, axis=0))
    return merge_heads(o)


def hgrn_lower_bounds(lb_logits):
    p = jax.nn.softmax(lb_logits.astype(jnp.float32), axis=0)
    return jnp.cumsum(p, axis=0) - p[0]


def hgrn2_chunkwise(q, k, v, log_f):
    b, h, s, kd = q.shape
    vd = v.shape[-1]
    c = HGRN_CHUNK
    n = s // c
    q = q.reshape(b, h, n, c, kd)
    k = k.reshape(b, h, n, c, kd)
    v = v.reshape(b, h, n, c, vd)
    g = jnp.cumsum(log_f.reshape(b, h, n, c, kd), axis=3)
    g_last = g[:, :, :, -1:]
    causal = jnp.tril(jnp.ones((c, c), dtype=bool))[:, :, None]
    diff = g[:, :, :, :, None, :] - g[:, :, :, None, :, :]
    decay = jnp.where(causal, jnp.exp(jnp.where(causal, diff, 0.0)), 0.0)
    a = jnp.einsum('bhnik,bhnjk,bhnijk->bhnij', q, k, decay)
    o_intra = jnp.einsum('bhnij,bhnjv->bhniv', a, v)
    q_in = q * jnp.exp(g)
    k_out = k * jnp.exp(g_last - g)
    chunk_decay = jnp.exp(g_last[:, :, :, 0])

    def step(state, xs):
        q_n, k_n, v_n, dec_n = xs
        o_n = jnp.einsum('bhik,bhkv->bhiv', q_n, state)
        state = dec_n[..., None] * state + jnp.einsum('bhjk,bhjv->bhkv', k_n, v_n)
        return state, o_n

    xs = (jnp.moveaxis(q_in, 2, 0), jnp.moveaxis(k_out, 2, 0),
          jnp.moveaxis(v, 2, 0), jnp.moveaxis(chunk_decay, 2, 0))
    state0 = jnp.zeros((b, h, kd, vd), dtype=jnp.float32)
    _, o_inter = lax.scan(step, state0, xs)
    o = o_intra + jnp.moveaxis(o_inter, 0, 2)
    return o.reshape(b, h, s, vd)


def hgrn2_group(q_h, f_h, i_h, g_h, lower_bound, out_gain):
    q = jax.nn.silu(split_heads(q_h, HGRN_HEADS, HGRN_HEAD_DIM).astype(jnp.float32))
    q = q * (HGRN_HEAD_DIM ** -0.5)
    z = split_heads(f_h, HGRN_HEADS, HGRN_HEAD_DIM).astype(jnp.float32)
    v = split_heads(i_h, HGRN_HEADS, HGRN_HEAD_DIM).astype(jnp.float32)
    lb = lower_bound.reshape(HGRN_HEADS, 1, HGRN_HEAD_DIM)
    log_f = jnp.log(lb + (1.0 - lb) * jax.nn.sigmoid(z))
    k = (1.0 - lb) * jax.nn.sigmoid(-z)
    o = hgrn2_chunkwise(q, k, v, log_f)
    o = rms_norm(o, out_gain)
    gate = jax.nn.silu(split_heads(g_h, HGRN_HEADS, HGRN_HEAD_DIM).astype(jnp.float32))
    return merge_heads(o * gate)


def setup_inputs(seed: int = 0) -> dict:
    key = jax.random.key(seed)
    ks = jax.random.split(key, 12)
    f32 = jnp.float32
    x = jax.random.normal(ks[0], (BATCH, SEQ, D_MODEL), f32)
    norm_mix = 1.0 + 0.02 * jax.random.normal(ks[1], (DEPTH, D_MODEL), f32)
    w_in = jax.random.normal(ks[2], (DEPTH, D_MODEL, IN_PROJ_WIDTH), f32) * D_MODEL ** -0.5
    attn_out_gain = 1.0 + 0.02 * jax.random.normal(ks[3], (DEPTH, ATTN_WIDTH), f32)
    hgrn_lb_logits = 0.1 * jax.random.normal(ks[4], (DEPTH, HGRN_WIDTH), f32)
    hgrn_out_gain = 1.0 + 0.02 * jax.random.normal(ks[5], (DEPTH, HGRN_HEAD_DIM), f32)
    w_out = jax.random.normal(ks[6], (DEPTH, MIX_WIDTH, D_MODEL), f32) * MIX_WIDTH ** -0.5
    norm_mlp = 1.0 + 0.02 * jax.random.normal(ks[7], (DEPTH, D_MODEL), f32)
    w_up = jax.random.normal(ks[8], (DEPTH, D_MODEL, MLP_HIDDEN), f32) * D_MODEL ** -0.5
    w_down = jax.random.normal(ks[9], (DEPTH, MLP_HIDDEN, D_MODEL), f32) * MLP_HIDDEN ** -0.5
    norm_final = 1.0 + 0.02 * jax.random.normal(ks[10], (D_MODEL,), f32)
    return {'x': x, 'norm_mix': norm_mix, 'w_in': w_in, 'attn_out_gain': attn_out_gain,
            'hgrn_lb_logits': hgrn_lb_logits, 'hgrn_out_gain': hgrn_out_gain,
            'w_out': w_out, 'norm_mlp': norm_mlp, 'w_up': w_up, 'w_down': w_down,
            'norm_final': norm_final}


def reference(x, norm_mix, w_in, attn_out_gain, hgrn_lb_logits, hgrn_out_gain,
              w_out, norm_mlp, w_up, w_down, norm_final):
    seq = x.shape[1]
    positions = jnp.arange(seq, dtype=jnp.int32)
    lower_bounds = hgrn_lower_bounds(hgrn_lb_logits)
    split_points = [ATTN_WIDTH, 2 * ATTN_WIDTH, 3 * ATTN_WIDTH,
                    3 * ATTN_WIDTH + HGRN_WIDTH, 3 * ATTN_WIDTH + 2 * HGRN_WIDTH,
                    3 * ATTN_WIDTH + 3 * HGRN_WIDTH]
    for layer in range(DEPTH):
        h = rms_norm(x, norm_mix[layer])
        proj = h @ w_in[layer]
        q_a, k_a, v_a, q_h, f_h, i_h, g_h = jnp.split(proj, split_points, axis=-1)
        attn = dilated_attention_group(q_a, k_a, v_a, positions)
        attn = rms_norm(attn, attn_out_gain[layer])
        rec = hgrn2_group(q_h, f_h, i_h, g_h, lower_bounds[layer], hgrn_out_gain[layer])
        mixed = jnp.concatenate([attn, rec], axis=-1).astype(x.dtype)
        x = x + mixed @ w_out[layer]
        h = rms_norm(x, norm_mlp[layer])
        x = x + jnp.square(jax.nn.relu(h @ w_up[layer])) @ w_down[layer]
    return rms_norm(x, norm_final)
```

```python
import contextlib
import numpy as np
import concourse.bass as bass
import concourse.mybir as mybir
from concourse.bass_utils import run_bass_kernel_spmd

F32 = mybir.dt.float32
BF16 = mybir.dt.bfloat16
AF = mybir.ActivationFunctionType
ALU = mybir.AluOpType

S, D, NL = 4096, 1024, 2
TT = 512
NT = S // TT
INW = 3584
HID = 4096
EPS = 1e-6
NEG = -30000.0
PATTERNS = (1, 4, 16)
DEBUG = False
SAME_SYNC = True


class Sem:
    def __init__(self, h):
        self.h = h
        self.c = 0


class Tl:
    __slots__ = ("w", "r", "name")

    def __init__(self, name=""):
        self.w = None
        self.r = {}
        self.name = name


class Buf:
    def __init__(self, t, name, nd=1):
        self.t = t
        self.d = Tl(name)
        self.ds = [Tl(f"{name}{i}") for i in range(nd)] if nd > 1 else None
        self.sem = None

    def __getitem__(self, k):
        return self.t[k]


class Prog:
    ENG = ["sp", "act", "pe", "dve", "pool"]

    def __init__(self, nc, es):
        self.nc = nc
        self.es = es
        self.stream = {e: [] for e in self.ENG}
        self.dsems = []
        self.esem = {e: self.newsem("e_" + e, False) for e in self.ENG}
        self.seen = {e: {} for e in self.ENG}
        self.nins = 0

    def newsem(self, name, dma=True):
        s = Sem(self.es.enter_context(self.nc.semaphore(name)))
        if dma:
            self.dsems.append(s)
        return s

    def _wait(self, eng, sem, val):
        if self.seen[eng].get(sem, 0) >= val:
            return
        self.seen[eng][sem] = val
        h = sem.h
        self.stream[eng].append(lambda e: e.wait_ge(h, val))

    def _sync(self, eng, reads, writes, skip_own=True):
        own = self.esem[eng] if (skip_own and (eng == "pe" or not SAME_SYNC)) else None
        for t in reads:
            if t.w is not None and t.w[0] is not own:
                self._wait(eng, *t.w)
        for t in writes:
            if t.w is not None and t.w[0] is not own:
                self._wait(eng, *t.w)
            for s, v in t.r.items():
                if s is not own:
                    self._wait(eng, s, v)

    def op(self, eng, fn, reads=(), writes=()):
        self._sync(eng, reads, writes)
        s = self.esem[eng]
        s.c += 1
        v = s.c
        h = s.h
        self.stream[eng].append(lambda e: fn(e).then_inc(h, 1))
        self.nins += 1
        for t in reads:
            t.r[s] = v
        for t in writes:
            t.w = (s, v)
            t.r = {}

    def dma(self, eng, out, in_, reads, writes, dsem):
        self._sync(eng, reads, writes, skip_own=False)
        dsem.c += 16
        v = dsem.c
        h = dsem.h
        self.stream[eng].append(lambda e: e.dma_start(out=out, in_=in_).then_inc(h, 16))
        self.nins += 1
        for t in reads:
            t.r[dsem] = v
        for t in writes:
            t.w = (dsem, v)
            t.r = {}

    def barrier(self):
        for e in self.ENG:
            for e2 in self.ENG:
                if e2 != e:
                    self._wait(e, self.esem[e2], self.esem[e2].c)
            for ds in self.dsems:
                self._wait(e, ds, ds.c)

    def flush(self):
        st = self.stream
        with self.nc.Block() as block:
            @block.sync
            def _(e):
                for f in st["sp"]:
                    f(e)

            @block.scalar
            def _(e):
                for f in st["act"]:
                    f(e)

            @block.tensor
            def _(e):
                for f in st["pe"]:
                    f(e)

            @block.vector
            def _(e):
                for f in st["dve"]:
                    f(e)

            @block.gpsimd
            def _(e):
                for f in st["pool"]:
                    f(e)
        self.stream = {e: [] for e in self.ENG}

    def mm(self, out, lhsT, rhs, start, stop, reads, writes):
        self.op("pe", lambda e: e.matmul(out, lhsT=lhsT, rhs=rhs, start=start, stop=stop), reads, writes)

    def tr(self, out, in_, ident, reads, writes):
        self.op("pe", lambda e: e.transpose(out, in_, ident), reads, writes)

    def act(self, out, in_, func, reads, writes, scale=None, bias=None, accum=None):
        kw = {}
        if scale is not None:
            kw["scale"] = scale
        if bias is not None:
            kw["bias"] = bias
        if accum is not None:
            kw["accum_out"] = accum
        self.op("act", lambda e: e.activation(out=out, in_=in_, func=func, **kw), reads, writes)

    def tt(self, eng, out, in0, in1, op, reads, writes):
        self.op(eng, lambda e: e.tensor_tensor(out=out, in0=in0, in1=in1, op=op), reads, writes)

    def ts(self, eng, out, in0, s1, s2, op0, op1, reads, writes):
        if s2 is None:
            self.op(eng, lambda e: e.tensor_scalar(out=out, in0=in0, scalar1=s1, scalar2=None, op0=op0), reads, writes)
        else:
            self.op(eng, lambda e: e.tensor_scalar(out=out, in0=in0, scalar1=s1, scalar2=s2, op0=op0, op1=op1), reads, writes)

    def stt(self, eng, out, in0, scalar, in1, op0, op1, reads, writes):
        self.op(eng, lambda e: e.scalar_tensor_tensor(out=out, in0=in0, scalar=scalar, in1=in1, op0=op0, op1=op1), reads, writes)

    def cp(self, eng, out, in_, reads, writes):
        if eng == "act":
            self.op("act", lambda e: e.activation(out=out, in_=in_, func=AF.Copy), reads, writes)
        else:
            self.op(eng, lambda e: e.tensor_copy(out=out, in_=in_), reads, writes)

    def memset(self, eng, ap, val, writes):
        self.op(eng, lambda e: e.memset(ap, val), (), writes)


def build():
    nc = bass.Bass("TRN2", target_bir_lowering=False)
    dt_in = lambda n, s: nc.dram_tensor(n, s, F32, kind="ExternalInput").ap()
    x_in = dt_in("x", [S, D])
    w_in = dt_in("w_in", [NL, D, INW])
    w_out = dt_in("w_out", [NL, D, D])
    w_up = dt_in("w_up", [NL, D, HID])
    w_down = dt_in("w_down", [NL, HID, D])
    p_nmix = dt_in("p_nmix", [128, NL, 8])
    p_again = dt_in("p_again", [128, NL, 4])
    p_lbl = dt_in("p_lbl", [128, NL, 4])
    p_hgain = dt_in("p_hgain", [128, NL, 1])
    p_nmlp = dt_in("p_nmlp", [128, NL, 8])
    p_nfin = dt_in("p_nfin", [128, D])
    c_cos = dt_in("c_cos", [128, S])
    c_sin = dt_in("c_sin", [128, S])
    c_rot = dt_in("c_rot", [128, 128])
    c_ident = dt_in("c_ident", [128, 128])
    c_mbias = dt_in("c_mbias", [128, 256])
    c_hmask = dt_in("c_hmask", [128, 256])
    y_out = nc.dram_tensor("y", [S, D], F32, kind="ExternalOutput").ap()

    kind_s = "ExternalOutput" if DEBUG else "Internal"
    scr = lambda n, s, d: nc.dram_tensor(n, s, d, kind=kind_s).ap()
    xres = scr("xres", [S, D], F32)
    qr_d = scr("qr_d", [4, 128, S], BF16)
    kr_d = scr("kr_d", [4, 128, S], BF16)
    va_d = scr("va_d", [S, 512], BF16)
    mix_d = scr("mix_d", [8, 128, S], BF16)
    wup_d = scr("wup_d", [128, 8, HID], BF16)
    wdn_d = scr("wdn_d", [128, 32, D], BF16)

    with contextlib.ExitStack() as es:
        P = Prog(nc, es)

        sfx = [""]

        def sb(es_, name, shape, dt, nd=1):
            name = name + sfx[0]
            return Buf(es_.enter_context(nc.sbuf_tensor(name, shape, dt)), name, nd)

        def ps(es_, name, shape, dt, nd=1):
            return Buf(es_.enter_context(nc.psum_tensor(name, shape, dt)), name, nd)

        def dsem(buf, nm=None):
            if buf.sem is None:
                buf.sem = P.newsem("d_" + (nm or buf.d.name))
            return buf.sem

        def load(buf, dst_ap, src_ap, eng="sp", tile=None):
            P.dma(eng, dst_ap, src_ap, [], [tile or buf.d], dsem(buf))

        ident = sb(es, "ident", [128, 128], BF16)
        rot = sb(es, "rot", [128, 128], BF16)
        mbias = sb(es, "mbias", [128, 256], BF16)
        hmask = sb(es, "hmask", [128, 256], F32)
        onesm = sb(es, "onesm", [128, 128], BF16)
        onec = sb(es, "onec", [128, 1], BF16)
        ones_f = sb(es, "ones_f", [128, 512], F32)
        eps_t = sb(es, "eps_t", [128, 1], F32)
        nmix = sb(es, "nmix", [128, NL, 8], F32)
        again = sb(es, "again", [128, NL, 4], F32)
        hgain = sb(es, "hgain", [128, NL, 1], F32)
        nmlp = sb(es, "nmlp", [128, NL, 8], F32)
        lbl = sb(es, "lbl", [128, NL, 4], F32)
        lb = sb(es, "lb", [128, NL, 4], F32)
        oml = sb(es, "oml", [128, NL, 4], F32)
        noml = sb(es, "noml", [128, NL, 4], F32)
        nomlc = sb(es, "nomlc", [128, NL, 4], F32)
        omlc = sb(es, "omlc", [128, NL, 4], F32)
        cst = sb(es, "cst", [128, 256], F32)
        Tst = [sb(es, f"Tst{h}", [128, 128], F32) for h in range(4)]
        Tbf = [sb(es, f"Tbf{h}", [128, 128], BF16) for h in range(4)]

        pb = [ps(es, f"pb{i}", [128, 512], F32) for i in range(7)]
        ptb = ps(es, "ptb", [128, 1024], BF16, nd=2)
        ptb.ds = [ptb.d, ptb.d]

        for i, (src, dstb) in enumerate([(c_ident, ident), (c_rot, rot)]):
            load(cst, cst[:, 0:128], src)
            P.cp("dve", dstb[:], cst[:, 0:128], [cst.d], [dstb.d])
        load(cst, cst[:, :], c_mbias)
        P.cp("dve", mbias[:], cst[:, :], [cst.d], [mbias.d])
        load(hmask, hmask[:], c_hmask)
        P.memset("pool", onesm[:], 1.0 / 128.0, [onesm.d])
        P.memset("pool", onec[:], 1.0, [onec.d])
        P.memset("pool", ones_f[:], 1.0, [ones_f.d])
        P.memset("pool", eps_t[:], EPS, [eps_t.d])
        for (src, dstb) in [(p_nmix, nmix), (p_again, again), (p_hgain, hgain), (p_nmlp, nmlp), (p_lbl, lbl)]:
            load(dstb, dstb[:], src)
        P.memset("dve", lb[:, 0, :], 0.0, [lb.d])
        P.tt("dve", lb[:, 1, :], lbl[:, 1, :], lbl[:, 0, :], ALU.subtract, [lbl.d], [lb.d])
        P.act(lb[:, 1, :], lb[:, 1, :], AF.Sigmoid, [lb.d], [lb.d])
        P.ts("dve", oml[:], lb[:], -1.0, 1.0, ALU.mult, ALU.add, [lb.d], [oml.d])
        P.ts("dve", noml[:], oml[:], -1.0, None, ALU.mult, None, [oml.d], [noml.d])
        P.ts("dve", omlc[:], oml[:], float(128 ** -0.5), None, ALU.mult, None, [oml.d], [omlc.d])
        P.ts("dve", nomlc[:], oml[:], -float(128 ** -0.5), None, ALU.mult, None, [oml.d], [nomlc.d])
        P.barrier()
        P.flush()

        rr_state = {"pmm": 0, "cast": 0}

        def cast_eng():
            rr_state["cast"] += 1
            return ("act", "dve", "pool")[rr_state["cast"] % 3]

        def cast(eng, out, in_, scale, reads, writes):
            if scale is None:
                P.cp(eng, out, in_, reads, writes)
            elif eng == "act":
                P.act(out, in_, AF.Copy, reads, writes, scale=scale)
            else:
                P.ts(eng, out, in_, scale, None, ALU.mult, None, reads, writes)

        for l in range(NL):
            x_src = x_in if l == 0 else xres
            sfx[0] = f"_L{l}"
            with contextlib.ExitStack() as ea:
                Win = sb(ea, "Win", [128, 8, INW], BF16)
                stg = [sb(ea, f"stgA{i}", [128, 1792], F32) for i in range(2)]
                xt = sb(ea, "xt", [128, 4, D], F32)
                xn = sb(ea, "xn", [128, 4, D], BF16, nd=4)
                hT = sb(ea, "hT", [128, 8, TT], BF16, nd=8)
                junk = sb(ea, "junk", [128, D], BF16)
                ss = sb(ea, "ss", [128, 4], F32)
                rt = sb(ea, "rt", [128, 4], F32)
                rs = sb(ea, "rs", [128, 4], F32)
                cs = sb(ea, "cs", [128, TT], F32)
                sn = sb(ea, "sn", [128, TT], F32)
                qsb = [sb(ea, f"qsb{i}", [128, TT], BF16) for i in range(2)]
                t1 = [sb(ea, f"t1_{i}", [128, TT], F32) for i in range(2)]
                t2 = [sb(ea, f"t2_{i}", [128, TT], F32) for i in range(2)]
                qst = sb(ea, "qst", [128, 4, TT], BF16)
                kst = sb(ea, "kst", [128, 4, TT], BF16)
                vst = sb(ea, "vst", [128, 4, 512], BF16)
                hv = sb(ea, "hv", [128, 4, 512], BF16)
                sil = sb(ea, "sil", [128, TT], F32)
                sg = sb(ea, "sg", [128, TT], F32)
                lf = sb(ea, "lf", [128, TT], F32)
                kk = sb(ea, "kk", [128, TT], F32)
                Lc = sb(ea, "Lc", [128, 8, 64], F32)
                Lst = sb(ea, "Lst", [128, 8], F32)
                Dd = [sb(ea, f"Dd{i}", [128, 8, 64], F32) for i in range(2)]
                Ee = [sb(ea, f"Ee{i}", [128, 8, 64], F32) for i in range(2)]
                dtmp = sb(ea, "dtmp", [128, 8], F32)
                dch = [sb(ea, f"dch{h}", [128, 8], F32) for h in range(4)]
                hqm = [sb(ea, f"hqm{h}", [128, TT], BF16) for h in range(4)]
                hqs = [sb(ea, f"hqs{h}", [128, TT], BF16) for h in range(4)]
                hkm = [sb(ea, f"hkm{h}", [128, TT], BF16) for h in range(4)]
                hke = [sb(ea, f"hke{h}", [128, TT], BF16) for h in range(4)]
                gate = [sb(ea, f"gate{h}", [128, TT], BF16) for h in range(4)]
                hketm = [sb(ea, f"hketm{h}", [128, 4, 128], BF16) for h in range(4)]
                am = [sb(ea, f"am{h}", [128, 256], BF16) for h in range(4)]
                osb = sb(ea, "osb", [128, TT], F32)
                sq = sb(ea, "sq", [128, TT], BF16)
                rno = sb(ea, "rno", [128, TT], F32)
                mst = sb(ea, "mst", [128, 4, TT], BF16)

                pi = 0
                for kc in range(8):
                    for hf in range(2):
                        st = stg[pi % 2]
                        load(st, st[:], w_in[l, kc * 128:(kc + 1) * 128, hf * 1792:(hf + 1) * 1792],
                             eng=("sp", "pool")[pi % 2])
                        cast(cast_eng(), Win[:, kc, hf * 1792:(hf + 1) * 1792], st[:], nmix[:, l, kc:kc + 1],
                             [st.d], [Win.d])
                        pi += 1
                for h in range(4):
                    P.memset("pool", Tst[h][:], 0.0, [Tst[h].d])
                    P.memset("pool", Tbf[h][:], 0.0, [Tbf[h].d])
                P.memset("pool", Lst[:, 0:1], 0.0, [Lst.d])

                pmm_i = [0]

                def next_pmm():
                    pmm_i[0] += 1
                    return pb[pmm_i[0] % 2]

                def fm_chunk(col0):
                    b = next_pmm()
                    for kc in range(8):
                        P.mm(b[:, :], Win[:, kc, col0:col0 + 128], hT[:, kc, :], kc == 0, kc == 7,
                             [Win.d, hT.ds[kc]], [b.d])
                    return b

                def tm_sub(col0, sub):
                    b = next_pmm()
                    for kc in range(8):
                        P.mm(b[:, :], hT[:, kc, sub * 128:(sub + 1) * 128], Win[:, kc, col0:col0 + 512], kc == 0,
                             kc == 7, [Win.d, hT.ds[kc]], [b.d])
                    return b

                c_hq = float(128 ** -0.5)
                for tt in range(NT):
                    t0 = tt * TT
                    load(xt, xt[:], x_src[t0:t0 + TT, :].rearrange("(s p) d -> p s d", p=128))
                    load(cs, cs[:], c_cos[:, t0:t0 + TT], eng="pool")
                    load(sn, sn[:], c_sin[:, t0:t0 + TT], eng="pool")
                    P.memset("dve", ss[:], 0.0, [ss.d])
                    for sub in range(4):
                        P.act(junk[:], xt[:, sub, :], AF.Square, [xt.d], [junk.d, ss.d], accum=ss[:, sub:sub + 1])
                    P.act(rt[:], ss[:], AF.Sqrt, [ss.d], [rt.d], scale=1.0 / D, bias=eps_t[:, 0:1])
                    P.op("dve", lambda e: e.reciprocal(out=rs[:], in_=rt[:]), [rt.d], [rs.d])
                    for sub in range(4):
                        eng = ("pool", "dve", "pool", "act")[sub]
                        cast(eng, xn[:, sub, :], xt[:, sub, :], rs[:, sub:sub + 1], [xt.d, rs.d], [xn.ds[sub]])
                    for kc in range(8):
                        hf = kc % 2
                        for sub in range(4):
                            P.tr(ptb[:, hf * 512 + sub * 128: hf * 512 + (sub + 1) * 128],
                                 xn[:, sub, kc * 128:(kc + 1) * 128], ident[:], [xn.ds[sub], ident.d], [ptb.ds[hf]])
                        P.cp(("act", "dve")[kc % 2], hT[:, kc, :], ptb[:, hf * 512:(hf + 1) * 512], [ptb.ds[hf]],
                             [hT.ds[kc]])

                    for hh in range(4):
                        b = fm_chunk(2048 + hh * 128)
                        P.act(sg[:], b[:, :], AF.Sigmoid, [b.d], [sg.d])
                        P.act(lf[:], sg[:], AF.Ln, [sg.d], [lf.d], scale=oml[:, l, hh:hh + 1], bias=lb[:, l, hh:hh + 1])
                        P.ts("dve", kk[:], sg[:], nomlc[:, l, hh:hh + 1], omlc[:, l, hh:hh + 1], ALU.mult, ALU.add,
                             [sg.d], [kk.d])
                        Lflat = Lc[:].rearrange("p c i -> p (c i)")
                        P.op("dve", lambda e, Lflat=Lflat: e.tensor_tensor_scan(out=Lflat, data0=ones_f[:], data1=lf[:],
                                                                                 initial=0.0, op0=ALU.mult, op1=ALU.add),
                             [ones_f.d, lf.d], [Lc.d])
                        P.cp("dve", Lst[:, 1:8], Lc[:, 0:7, 63], [Lc.d], [Lst.d])
                        P.tt("dve", dtmp[:], Lc[:, :, 63], Lst[:], ALU.subtract, [Lc.d, Lst.d], [dtmp.d])
                        P.act(dch[hh][:], dtmp[:], AF.Exp, [dtmp.d], [dch[hh].d])
                        b = fm_chunk(1536 + hh * 128)
                        P.act(sil[:], b[:, :], AF.Silu, [b.d], [sil.d])
                        bc = lambda ap: ap.broadcast_to([128, 8, 64])
                        sil3 = sil[:].rearrange("p (c i) -> p c i", i=64)
                        kk3 = kk[:].rearrange("p (c i) -> p c i", i=64)
                        v3 = lambda bf: bf[:].rearrange("p (c i) -> p c i", i=64)
                        D0, D1, E0, E1 = Dd[0], Dd[1], Ee[0], Ee[1]
                        P.tt("dve", D0[:], Lc[:], bc(Lc[:, :, 31:32]), ALU.subtract, [Lc.d], [D0.d])
                        P.act(E0[:], D0[:], AF.Exp, [D0.d], [E0.d])
                        P.act(E1[:], D0[:], AF.Exp, [D0.d], [E1.d], scale=-1.0)
                        P.tt("pool", v3(hqm[hh]), sil3, E0[:], ALU.mult, [sil.d, E0.d], [hqm[hh].d])
                        P.tt("pool", v3(hkm[hh]), kk3, E1[:], ALU.mult, [kk.d, E1.d], [hkm[hh].d])
                        P.tt("dve", D1[:], Lc[:], bc(Lst[:].rearrange("p (c o) -> p c o", o=1)), ALU.subtract,
                             [Lc.d, Lst.d], [D1.d])
                        P.act(E0[:], D1[:], AF.Exp, [D1.d], [E0.d])
                        P.tt("pool", v3(hqs[hh]), sil3, E0[:], ALU.mult, [sil.d, E0.d], [hqs[hh].d])
                        P.tt("dve", D0[:], bc(Lc[:, :, 63:64]), Lc[:], ALU.subtract, [Lc.d], [D0.d])
                        P.act(E1[:], D0[:], AF.Exp, [D0.d], [E1.d])
                        P.tt("pool", v3(hke[hh]), kk3, E1[:], ALU.mult, [kk.d, E1.d], [hke[hh].d])
                        b = fm_chunk(3072 + hh * 128)
                        P.act(gate[hh][:], b[:, :], AF.Silu, [b.d], [gate[hh].d])
                    for sub in range(4):
                        b = tm_sub(2560, sub)
                        P.cp(("act", "dve")[sub % 2], hv[:, sub, :], b[:, :], [b.d], [hv.d])

                    for which, (col0, stt_, scl) in enumerate([(0, qst, 0.125), (512, kst, None)]):
                        for m in range(4):
                            b = fm_chunk(col0 + m * 128)
                            qs_ = qsb[m % 2]
                            if scl is None:
                                P.cp("act", qs_[:], b[:, :], [b.d], [qs_.d])
                            else:
                                P.act(qs_[:], b[:, :], AF.Copy, [b.d], [qs_.d], scale=scl)
                            pr = pb[2]
                            P.mm(pr[:, :], rot[:], qs_[:], True, True, [rot.d, qs_.d], [pr.d])
                            a1, a2 = t1[m % 2], t2[m % 2]
                            P.tt("dve", a2[:], pr[:, :], sn[:], ALU.mult, [pr.d, sn.d], [a2.d])
                            P.tt("pool", a1[:], qs_[:], cs[:], ALU.mult, [qs_.d, cs.d], [a1.d])
                            P.tt("pool", stt_[:, m, :], a1[:], a2[:], ALU.add, [a1.d, a2.d], [stt_.d])
                        dst = (qr_d, kr_d)[which]
                        P.dma("sp", dst[:, :, t0:t0 + TT].rearrange("m p t -> p m t"), stt_[:], [stt_.d], [],
                              dsem(stt_))
                    for sub in range(4):
                        b = tm_sub(1024, sub)
                        P.cp(("act", "dve")[sub % 2], vst[:, sub, :], b[:, :], [b.d], [vst.d])
                    P.dma("sp", va_d[t0:t0 + TT, :].rearrange("(s p) c -> p s c", p=128), vst[:], [vst.d], [],
                          dsem(vst))

                    for hp in range(2):
                        heads = (2 * hp, 2 * hp + 1)
                        pA = pb[3]
                        pSb = {heads[0]: pb[3], heads[1]: pb[4]}
                        pO = {heads[0]: pb[5], heads[1]: pb[6]}
                        for hh in heads:
                            hf = hh % 2
                            for sub in range(4):
                                P.tr(ptb[:, hf * 512 + sub * 128: hf * 512 + (sub + 1) * 128],
                                     hke[hh][:, sub * 128:(sub + 1) * 128], ident[:], [hke[hh].d, ident.d],
                                     [ptb.ds[hf]])
                            P.cp("act", hketm[hh][:].rearrange("p s f -> p (s f)"), ptb[:, hf * 512:(hf + 1) * 512],
                                 [ptb.ds[hf]], [hketm[hh].d])
                            for sub in range(4):
                                for e_ in range(2):
                                    tk = sub * 128 + e_ * 64
                                    P.mm(pA[e_ * 64:(e_ + 1) * 64, hf * 256 + sub * 64: hf * 256 + (sub + 1) * 64],
                                         hkm[hh][:, tk:tk + 64], hqm[hh][:, tk:tk + 64], True, True,
                                         [hkm[hh].d, hqm[hh].d], [pA.d])
                            P.tt("dve", am[hh][:], pA[:, hf * 256:(hf + 1) * 256], hmask[:], ALU.mult,
                                 [pA.d, hmask.d], [am[hh].d])
                        for c in range(8):
                            sub, e_ = c // 2, c % 2
                            tk = sub * 128 + e_ * 64
                            rows = slice(e_ * 64, (e_ + 1) * 64)
                            for hh in heads:
                                slot = c % 4
                                pS = pSb[hh]
                                pSs = pS[:, slot * 128:(slot + 1) * 128]
                                vch = hv[rows, sub, hh * 128:(hh + 1) * 128]
                                po = pO[hh]
                                P.mm(po[:, c * 64:(c + 1) * 64], Tbf[hh][:], hqs[hh][:, tk:tk + 64], True, False,
                                     [Tbf[hh].d, hqs[hh].d], [po.d])
                                P.mm(po[:, c * 64:(c + 1) * 64], vch, am[hh][rows, sub * 64:(sub + 1) * 64], False, True,
                                     [hv.d, am[hh].d], [po.d])
                                P.mm(pSs, hketm[hh][rows, sub, :], vch, True, True, [hketm[hh].d, hv.d], [pS.d])
                                P.stt("dve", Tst[hh][:], Tst[hh][:], dch[hh][:, c:c + 1], pSs, ALU.mult, ALU.add,
                                      [pS.d, dch[hh].d], [Tst[hh].d])
                                P.cp("act", Tbf[hh][:], Tst[hh][:], [Tst[hh].d], [Tbf[hh].d])
                        for hh in heads:
                            po = pO[hh]
                            P.cp("act", osb[:], po[:, :], [po.d], [osb.d])
                            P.act(sq[:], po[:, :], AF.Square, [po.d], [sq.d])
                            pn = pb[2]
                            P.mm(pn[:, :], onesm[:], sq[:], True, True, [onesm.d, sq.d], [pn.d])
                            P.act(rno[:], pn[:, :], AF.Sqrt, [pn.d], [rno.d], bias=eps_t[:, 0:1])
                            P.op("dve", lambda e: e.reciprocal(out=rno[:], in_=rno[:]), [rno.d], [rno.d])
                            P.tt("pool", osb[:], osb[:], rno[:], ALU.mult, [rno.d], [osb.d])
                            P.tt("pool", mst[:, hh, :], osb[:], gate[hh][:], ALU.mult, [osb.d, gate[hh].d], [mst.d])
                    P.dma("sp", mix_d[4:8, :, t0:t0 + TT].rearrange("m p t -> p m t"), mst[:], [mst.d], [], dsem(mst))
                P.barrier()
                P.flush()

            with contextlib.ExitStack() as eb:
                q2 = sb(eb, "q2", [128, S], BF16)
                k2 = sb(eb, "k2", [128, S], BF16)
                qp = [q2] + [sb(eb, f"qp{i}", [128, S], BF16) for i in (1, 2)]
                kp = [k2] + [sb(eb, f"kp{i}", [128, S], BF16) for i in (1, 2)]
                vt = [[sb(eb, f"vt{j}_{i}", [128, 32, 128], BF16) for i in range(3)] for j in range(2)]
                acc = [sb(eb, f"acc{j}", [128, S], F32) for j in range(2)]
                PT = [sb(eb, f"PT{i}", [128, 256], BF16) for i in range(3)]
                lsh = sb(eb, "lsh", [64, S], F32)
                ost = [sb(eb, f"ost{j}", [64, S], BF16) for j in range(2)]
                wst = [sb(eb, f"wst{i}", [128, 1024], F32) for i in range(3)]
                wsb = [sb(eb, f"wsb{i}", [128, 1024], BF16) for i in range(3)]

                def prep_gen():
                    i = 0
                    for kc in range(8):
                        for q in range(4):
                            a, b_ = wst[i % 3], wsb[i % 3]
                            load(a, a[:], w_up[l, kc * 128:(kc + 1) * 128, q * 1024:(q + 1) * 1024], eng="sp")
                            cast("pool", b_[:], a[:], nmlp[:, l, kc:kc + 1], [a.d], [b_.d])
                            P.dma("sp", wup_d[:, kc, q * 1024:(q + 1) * 1024], b_[:], [b_.d], [], dsem(b_))
                            i += 1
                            yield
                    for hc in range(32):
                        a, b_ = wst[i % 3], wsb[i % 3]
                        load(a, a[:], w_down[l, hc * 128:(hc + 1) * 128, :], eng="sp")
                        cast("pool", b_[:], a[:], None, [a.d], [b_.d])
                        P.dma("sp", wdn_d[:, hc, :], b_[:], [b_.d], [], dsem(b_))
                        i += 1
                        yield

                pg = prep_gen()
                for j in range(2):
                    for i in range(3):
                        P.memset("pool", vt[j][i][:, :, 64:128], 1.0, [vt[j][i].d])
                psr = [pb[0], pb[1], pb[2]]
                por = [pb[3], pb[4], pb[5]]
                blk_i = 0
                for m in range(4):
                    load(q2, q2[:], qr_d[m])
                    load(k2, k2[:], kr_d[m], eng="pool")
                    for pi_, dl in enumerate(PATTERNS):
                        if dl == 1:
                            continue
                        P.cp("pool", qp[pi_][:], q2[:].rearrange("p (m r) -> p r m", r=dl), [q2.d], [qp[pi_].d])
                        P.cp("pool", kp[pi_][:], k2[:].rearrange("p (m r) -> p r m", r=dl), [k2.d], [kp[pi_].d])
                    for e_ in range(2):
                        h = 2 * m + e_
                        rows = slice(e_ * 64, (e_ + 1) * 64)
                        ac = acc[e_]
                        for pi_, dl in enumerate(PATTERNS):
                            v_ = vt[e_][pi_]
                            vsrc = va_d[:, h * 64:(h + 1) * 64].rearrange("(n j dl) c -> dl j n c", j=128, dl=dl)
                            nblk = 32 // dl
                            for r in range(dl):
                                P.dma("sp", v_[:, r * nblk:(r + 1) * nblk, 0:64], vsrc[r], [],
                                      [v_.d], dsem(v_))
                        for pi_, dl in enumerate(PATTERNS):
                            v_ = vt[e_][pi_]
                            nblk = 32 // dl
                            qq, kk_ = qp[pi_], kp[pi_]
                            acv = ac[:].rearrange("p (m r) -> p r m", r=dl)
                            for r in range(dl):
                                for n in range(nblk):
                                    bq = r * nblk + n
                                    nq = 256 if n < nblk - 1 else 128
                                    pS_ = psr[blk_i % 3]
                                    pt_ = PT[blk_i % 3]
                                    blk_i += 1
                                    P.mm(pS_[:, 0:nq], kk_[rows, bq * 128:(bq + 1) * 128],
                                         qq[rows, bq * 128:bq * 128 + nq], True, False, [kk_.d, qq.d], [pS_.d])
                                    P.mm(pS_[:, 0:nq], ident[:], mbias[:, 0:nq], False, True, [ident.d, mbias.d],
                                         [pS_.d])
                                    P.act(pt_[:, 0:nq], pS_[:, 0:nq], AF.Exp, [pS_.d], [pt_.d])
                                    g, sl = divmod(bq, 4)
                                    po = por[g % 3]
                                    P.mm(po[:, sl * 128:(sl + 1) * 128], v_[:, bq, :], pt_[:, 0:128], n == 0, True,
                                         [v_.d, pt_.d], [po.d])
                                    if n < nblk - 1:
                                        g2, sl2 = divmod(bq + 1, 4)
                                        po2 = por[g2 % 3]
                                        P.mm(po2[:, sl2 * 128:(sl2 + 1) * 128], v_[:, bq, :], pt_[:, 128:256], True,
                                             False, [v_.d, pt_.d], [po2.d])
                                    done_bank = (sl == 3) or (n == nblk - 1)
                                    if done_bank:
                                        first = max(g * 4, r * nblk)
                                        cnt = (bq - first + 1) * 128
                                        n_first = first - r * nblk
                                        dstv = acv[:, r, n_first * 128: n_first * 128 + cnt]
                                        srcv = po[:, (first - g * 4) * 128:(first - g * 4) * 128 + cnt]
                                        if pi_ == 0:
                                            P.cp("dve", dstv, srcv, [po.d], [ac.d])
                                        else:
                                            P.tt("dve", dstv, dstv, srcv, ALU.add, [po.d], [ac.d])
                                    if blk_i % 6 == 0:
                                        next(pg, None)
                        P.cp("dve", lsh[:], ac[64:128, :], [ac.d], [lsh.d])
                        P.op("dve", lambda e: e.reciprocal(out=lsh[:], in_=lsh[:]), [lsh.d], [lsh.d])
                        P.tt("pool", ost[e_][:], ac[0:64, :], lsh[:], ALU.mult, [ac.d, lsh.d], [ost[e_].d])
                        P.dma("sp", mix_d[m, e_ * 64:(e_ + 1) * 64, :], ost[e_][:], [ost[e_].d], [], dsem(ost[e_]))
                for _ in pg:
                    pass
                P.barrier()
                P.flush()

            with contextlib.ExitStack() as ec:
                Wo = sb(ec, "Wo", [128, 8, D], BF16)
                stg = [sb(ec, f"stgC{i}", [128, 1024], F32) for i in range(2)]
                xt = sb(ec, "xtC", [128, 4, D], F32)
                mt = sb(ec, "mt", [128, 8, TT], BF16)
                sqa = sb(ec, "sqa", [128, 4, TT], BF16)
                rsa = sb(ec, "rsa", [128, 4], F32)
                tmp = [sb(ec, f"tmpC{i}", [128, 512], F32) for i in range(2)]
                xn = sb(ec, "xnC", [128, 4, D], BF16, nd=4)
                hT = sb(ec, "hTC", [128, 8, TT], BF16, nd=8)
                junk = sb(ec, "junkC", [128, D], BF16)
                ss = sb(ec, "ssC", [128, 4], F32)
                rt = sb(ec, "rtC", [128, 4], F32)
                rs = sb(ec, "rsC", [128, 4], F32)
                uT = sb(ec, "uT", [128, 32, TT], BF16, nd=32)
                rl = [sb(ec, f"rl{i}", [128, TT], BF16) for i in range(2)]
                wu = [sb(ec, f"wu{i}", [128, 8, 512], BF16) for i in range(2)]
                wd = [sb(ec, f"wd{i}", [128, 4, 512], BF16) for i in range(3)]
                xo = sb(ec, "xo", [128, 4, D], F32)
                nfin = sb(ec, "nfin", [128, D], F32)
                last = (l == NL - 1)
                if last:
                    load(nfin, nfin[:], p_nfin)
                for kc in range(8):
                    st = stg[kc % 2]
                    load(st, st[:], w_out[l, kc * 128:(kc + 1) * 128, :], eng=("sp", "pool")[kc % 2])
                    gsc = again[:, l, kc:kc + 1] if kc < 4 else hgain[:, l, 0:1]
                    cast(cast_eng(), Wo[:, kc, :], st[:], gsc, [st.d], [Wo.d])
                gen_i = [0]

                def gbank():
                    gen_i[0] += 1
                    return pb[gen_i[0] % 3]

                for tt in range(NT):
                    t0 = tt * TT
                    load(xt, xt[:], x_src[t0:t0 + TT, :].rearrange("(s p) d -> p s d", p=128))
                    load(mt, mt[:], mix_d[:, :, t0:t0 + TT].rearrange("m p t -> p m t"), eng="pool")
                    P.act(sqa[:], mt[:, 0:4, :], AF.Square, [mt.d], [sqa.d])
                    pss = pb[6]
                    for sub in range(4):
                        for kc in range(4):
                            P.mm(pss[:, sub:sub + 1], sqa[:, kc, sub * 128:(sub + 1) * 128], onec[:], kc == 0, kc == 3,
                                 [sqa.d, onec.d], [pss.d])
                    P.act(rsa[:], pss[:, 0:4], AF.Sqrt, [pss.d], [rsa.d], scale=1.0 / 512.0, bias=eps_t[:, 0:1])
                    P.op("dve", lambda e: e.reciprocal(out=rsa[:], in_=rsa[:]), [rsa.d], [rsa.d])
                    for sub in range(4):
                        for nh in range(2):
                            pa = gbank()
                            for kc in range(4):
                                P.mm(pa[:, :], mt[:, kc, sub * 128:(sub + 1) * 128], Wo[:, kc, nh * 512:(nh + 1) * 512],
                                     kc == 0, kc == 3, [mt.d, Wo.d], [pa.d])
                            ph = gbank()
                            for kc in range(4, 8):
                                P.mm(ph[:, :], mt[:, kc, sub * 128:(sub + 1) * 128], Wo[:, kc, nh * 512:(nh + 1) * 512],
                                     kc == 4, kc == 7, [mt.d, Wo.d], [ph.d])
                            xs_ = xt[:, sub, nh * 512:(nh + 1) * 512]
                            tm_ = tmp[nh]
                            P.stt("dve", tm_[:], pa[:, :], rsa[:, sub:sub + 1], xs_, ALU.mult, ALU.add,
                                  [pa.d, rsa.d, xt.d], [tm_.d])
                            P.tt("dve", xs_, tm_[:], ph[:, :], ALU.add, [tm_.d, ph.d], [xt.d])
                    P.memset("dve", ss[:], 0.0, [ss.d])
                    for sub in range(4):
                        P.act(junk[:], xt[:, sub, :], AF.Square, [xt.d], [junk.d, ss.d], accum=ss[:, sub:sub + 1])
                    P.act(rt[:], ss[:], AF.Sqrt, [ss.d], [rt.d], scale=1.0 / D, bias=eps_t[:, 0:1])
                    P.op("dve", lambda e: e.reciprocal(out=rs[:], in_=rt[:]), [rt.d], [rs.d])
                    for sub in range(4):
                        eng = ("pool", "dve", "pool", "act")[sub]
                        cast(eng, xn[:, sub, :], xt[:, sub, :], rs[:, sub:sub + 1], [xt.d, rs.d], [xn.ds[sub]])
                    for kc in range(8):
                        hf = kc % 2
                        for sub in range(4):
                            P.tr(ptb[:, hf * 512 + sub * 128: hf * 512 + (sub + 1) * 128],
                                 xn[:, sub, kc * 128:(kc + 1) * 128], ident[:], [xn.ds[sub], ident.d], [ptb.ds[hf]])
                        P.cp(("act", "dve")[kc % 2], hT[:, kc, :], ptb[:, hf * 512:(hf + 1) * 512], [ptb.ds[hf]],
                             [hT.ds[kc]])
                    for g in range(8):
                        w_ = wu[g % 2]
                        load(w_, w_[:], wup_d[:, :, g * 512:(g + 1) * 512], eng=("sp", "pool")[g % 2])
                        for hl in range(4):
                            hc = g * 4 + hl
                            pu = gbank()
                            for kc in range(8):
                                P.mm(pu[:, :], w_[:, kc, hl * 128:(hl + 1) * 128], hT[:, kc, :], kc == 0, kc == 7,
                                     [w_.d, hT.ds[kc]], [pu.d])
                            r_ = rl[hc % 2]
                            P.act(r_[:], pu[:, :], AF.Relu, [pu.d], [r_.d])
                            P.tt(("pool", "dve")[hc % 2], uT[:, hc, :], r_[:], r_[:], ALU.mult, [r_.d], [uT.ds[hc]])
                    for nh in range(2):
                        py = [pb[3], pb[4], pb[5], pb[6]]
                        for g in range(8):
                            wi_ = (nh * 8 + g) % 3
                            w_ = wd[wi_]
                            load(w_, w_[:], wdn_d[:, g * 4:(g + 1) * 4, nh * 512:(nh + 1) * 512],
                                 eng=("sp", "pool", "sp")[wi_])
                            for sub in range(4):
                                for hl in range(4):
                                    hc = g * 4 + hl
                                    P.mm(py[sub][:, :], uT[:, hc, sub * 128:(sub + 1) * 128], w_[:, hl, :], hc == 0,
                                         hc == 31, [uT.ds[hc], w_.d], [py[sub].d])
                        for sub in range(4):
                            P.tt("dve", xo[:, sub, nh * 512:(nh + 1) * 512], xt[:, sub, nh * 512:(nh + 1) * 512],
                                 py[sub][:, :], ALU.add, [xt.d, py[sub].d], [xo.d])
                    if not last:
                        P.dma("sp", xres[t0:t0 + TT, :].rearrange("(s p) d -> p s d", p=128), xo[:], [xo.d], [],
                              dsem(xo))
                    else:
                        P.memset("dve", ss[:], 0.0, [ss.d])
                        for sub in range(4):
                            P.act(junk[:], xo[:, sub, :], AF.Square, [xo.d], [junk.d, ss.d], accum=ss[:, sub:sub + 1])
                        P.act(rt[:], ss[:], AF.Sqrt, [ss.d], [rt.d], scale=1.0 / D, bias=eps_t[:, 0:1])
                        P.op("dve", lambda e: e.reciprocal(out=rs[:], in_=rt[:]), [rt.d], [rs.d])
                        for sub in range(4):
                            P.stt("dve", xo[:, sub, :], xo[:, sub, :], rs[:, sub:sub + 1], nfin[:],
                                  ALU.mult, ALU.mult, [rs.d, nfin.d], [xo.d])
                        P.dma("sp", y_out[t0:t0 + TT, :].rearrange("(s p) d -> p s d", p=128), xo[:], [xo.d], [],
                              dsem(xo))
                P.barrier()
                P.flush()
    return nc


def _consts():
    f32 = np.float32
    half = 32
    inv = (10000.0 ** (-np.arange(half, dtype=f32) / half)).astype(f32)
    pos = np.arange(S, dtype=f32)
    ang = (pos[:, None] * inv[None, :]).astype(f32)
    cos, sin = np.cos(ang).astype(f32), np.sin(ang).astype(f32)
    fidx = (np.arange(128) % 64) % 32
    c_cos = np.ascontiguousarray(cos[:, fidx].T)
    c_sin = np.ascontiguousarray(sin[:, fidx].T)
    rot = np.zeros((128, 128), f32)
    for m in range(128):
        if (m % 64) < 32:
            rot[m + 32, m] = -1.0
        else:
            rot[m - 32, m] = 1.0
    ident = np.eye(128, dtype=f32)
    j = np.arange(128)[:, None]
    i = np.arange(128)[None, :]
    mb = np.concatenate([np.where(j <= i, 0.0, NEG), np.where(j >= i, 0.0, NEG)], axis=1).astype(f32)
    jj = (np.arange(128) % 64)[:, None]
    ii = np.arange(64)[None, :]
    hm = np.tile((jj <= ii).astype(f32), (1, 4))
    return dict(c_cos=c_cos, c_sin=c_sin, c_rot=rot, c_ident=ident, c_mbias=mb, c_hmask=hm)


_CACHE = {}


def kernel(x, norm_mix, w_in, attn_out_gain, hgrn_lb_logits, hgrn_out_gain, w_out, norm_mlp, w_up, w_down,
           norm_final):
    f32 = np.float32
    a = lambda v: np.ascontiguousarray(np.asarray(v, dtype=f32))
    if "nc" not in _CACHE:
        _CACHE["nc"] = build()
    nc = _CACHE["nc"]
    col = lambda v, k: np.ascontiguousarray(a(v).reshape(NL, k, 128).transpose(2, 0, 1))
    shared = dict(
        w_in=a(w_in), w_out=a(w_out), w_up=a(w_up), w_down=a(w_down),
        p_nmix=col(norm_mix, 8), p_again=col(attn_out_gain, 4), p_lbl=col(hgrn_lb_logits, 4),
        p_hgain=col(hgrn_out_gain, 1), p_nmlp=col(norm_mlp, 8),
        p_nfin=np.ascontiguousarray(np.broadcast_to(a(norm_final)[None, :], (128, D))),
    )
    shared.update(_consts())
    xs = a(x)
    in_maps = [dict(shared, x=np.ascontiguousarray(xs[b])) for b in range(8)]
    res = run_bass_kernel_spmd(nc, in_maps, core_ids=list(range(8)))
    _CACHE["res"] = res
    return np.stack([r["y"] for r in res.results], axis=0).astype(f32)
```

```python
import contextlib
import numpy as np
import concourse.bass as bass
import concourse.mybir as mybir
from concourse.bass_utils import run_bass_kernel_spmd

F32 = mybir.dt.float32
BF16 = mybir.dt.bfloat16
AF = mybir.ActivationFunctionType
ALU = mybir.AluOpType

S, D, NL = 4096, 1024, 2
TT = 512
NT = S // TT
INW = 3584
HID = 4096
EPS = 1e-6
NEG = -30000.0
PATTERNS = (1, 4, 16)
DEBUG = False
SAME_SYNC = True


class Sem:
    def __init__(self, h):
        self.h = h
        self.c = 0


class Tl:
    __slots__ = ("w", "r", "name", "small")

    def __init__(self, name="", small=False):
        self.w = None
        self.r = {}
        self.name = name
        self.small = small


class Buf:
    def __init__(self, t, name, nd=1, small=False):
        self.t = t
        self.d = Tl(name, small)
        self.ds = [Tl(f"{name}{i}", small) for i in range(nd)] if nd > 1 else None
        self.sem = None

    def __getitem__(self, k):
        return self.t[k]


class Prog:
    ENG = ["sp", "act", "pe", "dve", "pool"]

    def __init__(self, nc, es):
        self.nc = nc
        self.es = es
        self.stream = {e: [] for e in self.ENG}
        self.dsems = []
        self.free_sems = {}
        self.phase_sems = []
        self.esem = {e: self.newsem("e_" + e, False) for e in self.ENG}
        self.seen = {e: {} for e in self.ENG}
        self.nins = 0

    def newsem(self, name, dma=True, q="sp"):
        if dma and self.free_sems.get(q):
            s = self.free_sems[q].pop()
        else:
            s = Sem(self.es.enter_context(self.nc.semaphore(name)))
            if dma:
                self.dsems.append(s)
        if dma:
            self.phase_sems.append((q, s))
        return s

    def recycle(self):
        for q, s in self.phase_sems:
            self.free_sems.setdefault(q, []).append(s)
        self.phase_sems = []

    def _wait(self, eng, sem, val):
        if self.seen[eng].get(sem, 0) >= val:
            return
        self.seen[eng][sem] = val
        h = sem.h
        self.stream[eng].append(lambda e: e.wait_ge(h, val))

    def _sync(self, eng, reads, writes, skip_own=True):
        own = self.esem[eng] if (skip_own and (eng == "pe" or not SAME_SYNC)) else None
        for t in reads:
            if t.w is not None and (t.w[0] is not own or (t.small and eng != "pe")):
                self._wait(eng, *t.w)
        for t in writes:
            if t.w is not None and (t.w[0] is not own or (t.small and eng != "pe")):
                self._wait(eng, *t.w)
            for s, v in t.r.items():
                if s is not own:
                    self._wait(eng, s, v)

    def op(self, eng, fn, reads=(), writes=()):
        self._sync(eng, reads, writes)
        s = self.esem[eng]
        s.c += 1
        v = s.c
        h = s.h
        self.stream[eng].append(lambda e: fn(e).then_inc(h, 1))
        self.nins += 1
        for t in reads:
            t.r[s] = v
        for t in writes:
            t.w = (s, v)
            t.r = {}

    def dma(self, eng, out, in_, reads, writes, dsem):
        self._sync(eng, reads, writes, skip_own=False)
        dsem.c += 16
        v = dsem.c
        h = dsem.h
        self.stream[eng].append(lambda e: e.dma_start(out=out, in_=in_).then_inc(h, 16))
        self.nins += 1
        for t in reads:
            t.r[dsem] = v
        for t in writes:
            t.w = (dsem, v)
            t.r = {}

    def barrier(self):
        for e in self.ENG:
            for e2 in self.ENG:
                if e2 != e:
                    self._wait(e, self.esem[e2], self.esem[e2].c)
            for ds in self.dsems:
                self._wait(e, ds, ds.c)

    def flush(self):
        st = self.stream
        with self.nc.Block() as block:
            @block.sync
            def _(e):
                for f in st["sp"]:
                    f(e)

            @block.scalar
            def _(e):
                for f in st["act"]:
                    f(e)

            @block.tensor
            def _(e):
                for f in st["pe"]:
                    f(e)

            @block.vector
            def _(e):
                for f in st["dve"]:
                    f(e)

            @block.gpsimd
            def _(e):
                for f in st["pool"]:
                    f(e)
        self.stream = {e: [] for e in self.ENG}

    def mm(self, out, lhsT, rhs, start, stop, reads, writes):
        self.op("pe", lambda e: e.matmul(out, lhsT=lhsT, rhs=rhs, start=start, stop=stop), reads, writes)

    def tr(self, out, in_, ident, reads, writes):
        self.op("pe", lambda e: e.transpose(out, in_, ident), reads, writes)

    def act(self, out, in_, func, reads, writes, scale=None, bias=None, accum=None):
        kw = {}
        if scale is not None:
            kw["scale"] = scale
        if bias is not None:
            kw["bias"] = bias
        if accum is not None:
            kw["accum_out"] = accum
        self.op("act", lambda e: e.activation(out=out, in_=in_, func=func, **kw), reads, writes)

    def tt(self, eng, out, in0, in1, op, reads, writes):
        self.op(eng, lambda e: e.tensor_tensor(out=out, in0=in0, in1=in1, op=op), reads, writes)

    def ts(self, eng, out, in0, s1, s2, op0, op1, reads, writes):
        if s2 is None:
            self.op(eng, lambda e: e.tensor_scalar(out=out, in0=in0, scalar1=s1, scalar2=None, op0=op0), reads, writes)
        else:
            self.op(eng, lambda e: e.tensor_scalar(out=out, in0=in0, scalar1=s1, scalar2=s2, op0=op0, op1=op1), reads, writes)

    def stt(self, eng, out, in0, scalar, in1, op0, op1, reads, writes):
        self.op(eng, lambda e: e.scalar_tensor_tensor(out=out, in0=in0, scalar=scalar, in1=in1, op0=op0, op1=op1), reads, writes)

    def cp(self, eng, out, in_, reads, writes):
        if eng == "act":
            self.op("act", lambda e: e.activation(out=out, in_=in_, func=AF.Copy), reads, writes)
        else:
            self.op(eng, lambda e: e.tensor_copy(out=out, in_=in_), reads, writes)

    def memset(self, eng, ap, val, writes):
        self.op(eng, lambda e: e.memset(ap, val), (), writes)


def build():
    nc = bass.Bass("TRN2", target_bir_lowering=False)
    dt_in = lambda n, s: nc.dram_tensor(n, s, F32, kind="ExternalInput").ap()
    x_in = dt_in("x", [S, D])
    w_in = dt_in("w_in", [NL, D, INW])
    w_out = dt_in("w_out", [NL, D, D])
    w_up = dt_in("w_up", [NL, D, HID])
    w_down = dt_in("w_down", [NL, HID, D])
    p_nmix = dt_in("p_nmix", [128, NL, 8])
    p_again = dt_in("p_again", [128, NL, 4])
    p_lbl = dt_in("p_lbl", [128, NL, 4])
    p_hgain = dt_in("p_hgain", [128, NL, 1])
    p_nmlp = dt_in("p_nmlp", [128, NL, 8])
    p_nfin = dt_in("p_nfin", [128, D])
    c_cos = dt_in("c_cos", [128, S])
    c_sin = dt_in("c_sin", [128, S])
    c_rot = dt_in("c_rot", [128, 128])
    c_ident = dt_in("c_ident", [128, 128])
    c_mbias = dt_in("c_mbias", [128, 256])
    c_hmask = dt_in("c_hmask", [128, 256])
    y_out = nc.dram_tensor("y", [S, D], F32, kind="ExternalOutput").ap()

    kind_s = "ExternalOutput" if DEBUG else "Internal"
    scr = lambda n, s, d: nc.dram_tensor(n, s, d, kind=kind_s).ap()
    xres = scr("xres", [S, D], F32)
    qr_d = scr("qr_d", [4, 128, S], BF16)
    kr_d = scr("kr_d", [4, 128, S], BF16)
    va_d = scr("va_d", [S, 512], BF16)
    mix_d = scr("mix_d", [8, 128, S], BF16)
    wup_d = scr("wup_d", [128, 8, HID], BF16)
    wdn_d = scr("wdn_d", [128, 32, D], BF16)

    with contextlib.ExitStack() as es:
        P = Prog(nc, es)

        sfx = [""]

        def sb(es_, name, shape, dt, nd=1):
            name = name + sfx[0]
            small = int(np.prod(shape[1:])) <= 256
            return Buf(es_.enter_context(nc.sbuf_tensor(name, shape, dt)), name, nd, small)

        def ps(es_, name, shape, dt, nd=1):
            return Buf(es_.enter_context(nc.psum_tensor(name, shape, dt)), name, nd)

        def dsem(buf, q="sp"):
            if buf.sem is None:
                buf.sem = P.newsem("d_" + buf.d.name, True, q)
                buf.semq = q
            assert buf.semq == q, (buf.d.name, buf.semq, q)
            return buf.sem

        def load(buf, dst_ap, src_ap, eng="sp", tile=None):
            P.dma(eng, dst_ap, src_ap, [], [tile or buf.d], dsem(buf, eng))

        ident = sb(es, "ident", [128, 128], BF16)
        rot = sb(es, "rot", [128, 128], BF16)
        mbias = sb(es, "mbias", [128, 256], BF16)
        hmask = sb(es, "hmask", [128, 256], F32)
        onesm = sb(es, "onesm", [128, 128], BF16)
        onec = sb(es, "onec", [128, 1], BF16)
        ones_f = sb(es, "ones_f", [128, 512], F32)
        eps_t = sb(es, "eps_t", [128, 1], F32)
        nmix = sb(es, "nmix", [128, NL, 8], F32)
        again = sb(es, "again", [128, NL, 4], F32)
        hgain = sb(es, "hgain", [128, NL, 1], F32)
        nmlp = sb(es, "nmlp", [128, NL, 8], F32)
        lbl = sb(es, "lbl", [128, NL, 4], F32)
        lb = sb(es, "lb", [128, NL, 4], F32)
        oml = sb(es, "oml", [128, NL, 4], F32)
        noml = sb(es, "noml", [128, NL, 4], F32)
        nomlc = sb(es, "nomlc", [128, NL, 4], F32)
        omlc = sb(es, "omlc", [128, NL, 4], F32)
        cst = sb(es, "cst", [128, 256], F32)
        Tst = [sb(es, f"Tst{h}", [128, 128], F32) for h in range(4)]
        Tbf = [sb(es, f"Tbf{h}", [128, 128], BF16) for h in range(4)]

        pb = [ps(es, f"pb{i}", [128, 512], F32) for i in range(7)]
        ptb = ps(es, "ptb", [128, 1024], BF16, nd=2)
        ptb.ds = [ptb.d, ptb.d]

        for i, (src, dstb) in enumerate([(c_ident, ident), (c_rot, rot)]):
            load(cst, cst[:, 0:128], src)
            P.cp("dve", dstb[:], cst[:, 0:128], [cst.d], [dstb.d])
        load(cst, cst[:, :], c_mbias)
        P.cp("dve", mbias[:], cst[:, :], [cst.d], [mbias.d])
        load(hmask, hmask[:], c_hmask)
        P.memset("pool", onesm[:], 1.0 / 128.0, [onesm.d])
        P.memset("pool", onec[:], 1.0, [onec.d])
        P.memset("pool", ones_f[:], 1.0, [ones_f.d])
        P.memset("pool", eps_t[:], EPS, [eps_t.d])
        for (src, dstb) in [(p_nmix, nmix), (p_again, again), (p_hgain, hgain), (p_nmlp, nmlp), (p_lbl, lbl)]:
            load(dstb, dstb[:], src)
        P.memset("dve", lb[:, 0, :], 0.0, [lb.d])
        P.tt("dve", lb[:, 1, :], lbl[:, 1, :], lbl[:, 0, :], ALU.subtract, [lbl.d], [lb.d])
        P.act(lb[:, 1, :], lb[:, 1, :], AF.Sigmoid, [lb.d], [lb.d])
        P.ts("dve", oml[:], lb[:], -1.0, 1.0, ALU.mult, ALU.add, [lb.d], [oml.d])
        P.ts("dve", noml[:], oml[:], -1.0, None, ALU.mult, None, [oml.d], [noml.d])
        P.ts("dve", omlc[:], oml[:], float(128 ** -0.5), None, ALU.mult, None, [oml.d], [omlc.d])
        P.ts("dve", nomlc[:], oml[:], -float(128 ** -0.5), None, ALU.mult, None, [oml.d], [nomlc.d])
        P.barrier()
        P.flush()

        rr_state = {"pmm": 0, "cast": 0}

        def cast_eng():
            rr_state["cast"] += 1
            return ("act", "dve", "pool")[rr_state["cast"] % 3]

        def cast(eng, out, in_, scale, reads, writes):
            if scale is None:
                P.cp(eng, out, in_, reads, writes)
            elif eng == "act":
                P.act(out, in_, AF.Copy, reads, writes, scale=scale)
            else:
                P.ts(eng, out, in_, scale, None, ALU.mult, None, reads, writes)

        for l in range(NL):
            x_src = x_in if l == 0 else xres
            sfx[0] = f"_L{l}"
            with contextlib.ExitStack() as ea:
                Win = sb(ea, "Win", [128, 8, INW], BF16)
                stg = [sb(ea, f"stgA{i}", [128, 1792], F32) for i in range(2)]
                xt = sb(ea, "xt", [128, 4, D], F32)
                xn = sb(ea, "xn", [128, 4, D], BF16, nd=4)
                hT = sb(ea, "hT", [128, 8, TT], BF16, nd=8)
                junk = sb(ea, "junk", [128, D], BF16)
                ss = sb(ea, "ss", [128, 4], F32)
                rt = sb(ea, "rt", [128, 4], F32)
                rs = sb(ea, "rs", [128, 4], F32)
                cs = sb(ea, "cs", [128, TT], F32)
                sn = sb(ea, "sn", [128, TT], F32)
                qsb = [sb(ea, f"qsb{i}", [128, TT], BF16) for i in range(2)]
                t1 = [sb(ea, f"t1_{i}", [128, TT], F32) for i in range(2)]
                t2 = [sb(ea, f"t2_{i}", [128, TT], F32) for i in range(2)]
                qst = sb(ea, "qst", [128, 4, TT], BF16)
                kst = sb(ea, "kst", [128, 4, TT], BF16)
                vst = sb(ea, "vst", [128, 4, 512], BF16)
                hv = sb(ea, "hv", [128, 4, 512], BF16)
                sil = sb(ea, "sil", [128, TT], F32)
                sg = sb(ea, "sg", [128, TT], F32)
                lf = sb(ea, "lf", [128, TT], F32)
                kk = sb(ea, "kk", [128, TT], F32)
                Lc = sb(ea, "Lc", [128, 8, 64], F32)
                Lst = sb(ea, "Lst", [128, 8], F32)
                Dd = [sb(ea, f"Dd{i}", [128, 8, 64], F32) for i in range(2)]
                Ee = [sb(ea, f"Ee{i}", [128, 8, 64], F32) for i in range(2)]
                dtmp = sb(ea, "dtmp", [128, 8], F32)
                dch = [sb(ea, f"dch{h}", [128, 8], F32) for h in range(4)]
                hqm = [sb(ea, f"hqm{h}", [128, TT], BF16) for h in range(4)]
                hqs = [sb(ea, f"hqs{h}", [128, TT], BF16) for h in range(4)]
                hkm = [sb(ea, f"hkm{h}", [128, TT], BF16) for h in range(4)]
                hke = [sb(ea, f"hke{h}", [128, TT], BF16) for h in range(4)]
                gate = [sb(ea, f"gate{h}", [128, TT], BF16) for h in range(4)]
                hketm = [sb(ea, f"hketm{h}", [128, 4, 128], BF16) for h in range(4)]
                am = [sb(ea, f"am{h}", [128, 256], BF16) for h in range(4)]
                osb = sb(ea, "osb", [128, TT], F32)
                sq = sb(ea, "sq", [128, TT], BF16)
                rno = sb(ea, "rno", [128, TT], F32)
                mst = sb(ea, "mst", [128, 4, TT], BF16)

                pi = 0
                for kc in range(8):
                    for hf in range(2):
                        st = stg[pi % 2]
                        load(st, st[:], w_in[l, kc * 128:(kc + 1) * 128, hf * 1792:(hf + 1) * 1792],
                             eng=("sp", "pool")[pi % 2])
                        cast(cast_eng(), Win[:, kc, hf * 1792:(hf + 1) * 1792], st[:], nmix[:, l, kc:kc + 1],
                             [st.d], [Win.d])
                        pi += 1
                for h in range(4):
                    P.memset("pool", Tst[h][:], 0.0, [Tst[h].d])
                    P.memset("pool", Tbf[h][:], 0.0, [Tbf[h].d])
                P.memset("pool", Lst[:, 0:1], 0.0, [Lst.d])

                pmm_i = [0]

                def next_pmm():
                    pmm_i[0] += 1
                    return pb[pmm_i[0] % 2]

                def fm_chunk(col0):
                    b = next_pmm()
                    for kc in range(8):
                        P.mm(b[:, :], Win[:, kc, col0:col0 + 128], hT[:, kc, :], kc == 0, kc == 7,
                             [Win.d, hT.ds[kc]], [b.d])
                    return b

                def tm_sub(col0, sub):
                    b = next_pmm()
                    for kc in range(8):
                        P.mm(b[:, :], hT[:, kc, sub * 128:(sub + 1) * 128], Win[:, kc, col0:col0 + 512], kc == 0,
                             kc == 7, [Win.d, hT.ds[kc]], [b.d])
                    return b

                c_hq = float(128 ** -0.5)
                for tt in range(NT):
                    t0 = tt * TT
                    load(xt, xt[:], x_src[t0:t0 + TT, :].rearrange("(s p) d -> p s d", p=128))
                    load(cs, cs[:], c_cos[:, t0:t0 + TT], eng="pool")
                    load(sn, sn[:], c_sin[:, t0:t0 + TT], eng="pool")
                    P.memset("dve", ss[:], 0.0, [ss.d])
                    for sub in range(4):
                        P.act(junk[:], xt[:, sub, :], AF.Square, [xt.d], [junk.d, ss.d], accum=ss[:, sub:sub + 1])
                    P.act(rt[:], ss[:], AF.Sqrt, [ss.d], [rt.d], scale=1.0 / D, bias=eps_t[:, 0:1])
                    P.op("dve", lambda e: e.reciprocal(out=rs[:], in_=rt[:]), [rt.d], [rs.d])
                    for sub in range(4):
                        eng = ("pool", "dve", "pool", "act")[sub]
                        cast(eng, xn[:, sub, :], xt[:, sub, :], rs[:, sub:sub + 1], [xt.d, rs.d], [xn.ds[sub]])
                    for kc in range(8):
                        hf = kc % 2
                        for sub in range(4):
                            P.tr(ptb[:, hf * 512 + sub * 128: hf * 512 + (sub + 1) * 128],
                                 xn[:, sub, kc * 128:(kc + 1) * 128], ident[:], [xn.ds[sub], ident.d], [ptb.ds[hf]])
                        P.cp(("act", "dve")[kc % 2], hT[:, kc, :], ptb[:, hf * 512:(hf + 1) * 512], [ptb.ds[hf]],
                             [hT.ds[kc]])

                    for hh in range(4):
                        b = fm_chunk(2048 + hh * 128)
                        P.act(sg[:], b[:, :], AF.Sigmoid, [b.d], [sg.d])
                        P.act(lf[:], sg[:], AF.Ln, [sg.d], [lf.d], scale=oml[:, l, hh:hh + 1], bias=lb[:, l, hh:hh + 1])
                        P.ts("dve", kk[:], sg[:], nomlc[:, l, hh:hh + 1], omlc[:, l, hh:hh + 1], ALU.mult, ALU.add,
                             [sg.d], [kk.d])
                        Lflat = Lc[:].rearrange("p c i -> p (c i)")
                        P.op("dve", lambda e, Lflat=Lflat: e.tensor_tensor_scan(out=Lflat, data0=ones_f[:], data1=lf[:],
                                                                                 initial=0.0, op0=ALU.mult, op1=ALU.add),
                             [ones_f.d, lf.d], [Lc.d])
                        P.cp("dve", Lst[:, 1:8], Lc[:, 0:7, 63], [Lc.d], [Lst.d])
                        P.tt("dve", dtmp[:], Lc[:, :, 63], Lst[:], ALU.subtract, [Lc.d, Lst.d], [dtmp.d])
                        P.act(dch[hh][:], dtmp[:], AF.Exp, [dtmp.d], [dch[hh].d])
                        b = fm_chunk(1536 + hh * 128)
                        P.act(sil[:], b[:, :], AF.Silu, [b.d], [sil.d])
                        bc = lambda ap: ap.broadcast_to([128, 8, 64])
                        sil3 = sil[:].rearrange("p (c i) -> p c i", i=64)
                        kk3 = kk[:].rearrange("p (c i) -> p c i", i=64)
                        v3 = lambda bf: bf[:].rearrange("p (c i) -> p c i", i=64)
                        D0, D1, E0, E1 = Dd[0], Dd[1], Ee[0], Ee[1]
                        P.tt("dve", D0[:], Lc[:], bc(Lc[:, :, 31:32]), ALU.subtract, [Lc.d], [D0.d])
                        P.act(E0[:], D0[:], AF.Exp, [D0.d], [E0.d])
                        P.act(E1[:], D0[:], AF.Exp, [D0.d], [E1.d], scale=-1.0)
                        P.tt("pool", v3(hqm[hh]), sil3, E0[:], ALU.mult, [sil.d, E0.d], [hqm[hh].d])
                        P.tt("pool", v3(hkm[hh]), kk3, E1[:], ALU.mult, [kk.d, E1.d], [hkm[hh].d])
                        P.tt("dve", D1[:], Lc[:], bc(Lst[:].rearrange("p (c o) -> p c o", o=1)), ALU.subtract,
                             [Lc.d, Lst.d], [D1.d])
                        P.act(E0[:], D1[:], AF.Exp, [D1.d], [E0.d])
                        P.tt("pool", v3(hqs[hh]), sil3, E0[:], ALU.mult, [sil.d, E0.d], [hqs[hh].d])
                        P.tt("dve", D0[:], bc(Lc[:, :, 63:64]), Lc[:], ALU.subtract, [Lc.d], [D0.d])
                        P.act(E1[:], D0[:], AF.Exp, [D0.d], [E1.d])
                        P.tt("pool", v3(hke[hh]), kk3, E1[:], ALU.mult, [kk.d, E1.d], [hke[hh].d])
                        b = fm_chunk(3072 + hh * 128)
                        P.act(gate[hh][:], b[:, :], AF.Silu, [b.d], [gate[hh].d])
                    for sub in range(4):
                        b = tm_sub(2560, sub)
                        P.cp(("act", "dve")[sub % 2], hv[:, sub, :], b[:, :], [b.d], [hv.d])

                    for which, (col0, stt_, scl) in enumerate([(0, qst, 0.125), (512, kst, None)]):
                        for m in range(4):
                            b = fm_chunk(col0 + m * 128)
                            qs_ = qsb[m % 2]
                            if scl is None:
                                P.cp("act", qs_[:], b[:, :], [b.d], [qs_.d])
                            else:
                                P.act(qs_[:], b[:, :], AF.Copy, [b.d], [qs_.d], scale=scl)
                            pr = pb[2]
                            P.mm(pr[:, :], rot[:], qs_[:], True, True, [rot.d, qs_.d], [pr.d])
                            a1, a2 = t1[m % 2], t2[m % 2]
                            P.tt("dve", a2[:], pr[:, :], sn[:], ALU.mult, [pr.d, sn.d], [a2.d])
                            P.tt("pool", a1[:], qs_[:], cs[:], ALU.mult, [qs_.d, cs.d], [a1.d])
                            P.tt("pool", stt_[:, m, :], a1[:], a2[:], ALU.add, [a1.d, a2.d], [stt_.d])
                        dst = (qr_d, kr_d)[which]
                        P.dma("sp", dst[:, :, t0:t0 + TT].rearrange("m p t -> p m t"), stt_[:], [stt_.d], [],
                              dsem(stt_))
                    for sub in range(4):
                        b = tm_sub(1024, sub)
                        P.cp(("act", "dve")[sub % 2], vst[:, sub, :], b[:, :], [b.d], [vst.d])
                    P.dma("sp", va_d[t0:t0 + TT, :].rearrange("(s p) c -> p s c", p=128), vst[:], [vst.d], [],
                          dsem(vst))

                    for hp in range(2):
                        heads = (2 * hp, 2 * hp + 1)
                        pA = pb[3]
                        pSb = {heads[0]: pb[3], heads[1]: pb[4]}
                        pO = {heads[0]: pb[5], heads[1]: pb[6]}
                        for hh in heads:
                            hf = hh % 2
                            for sub in range(4):
                                P.tr(ptb[:, hf * 512 + sub * 128: hf * 512 + (sub + 1) * 128],
                                     hke[hh][:, sub * 128:(sub + 1) * 128], ident[:], [hke[hh].d, ident.d],
                                     [ptb.ds[hf]])
                            P.cp("act", hketm[hh][:].rearrange("p s f -> p (s f)"), ptb[:, hf * 512:(hf + 1) * 512],
                                 [ptb.ds[hf]], [hketm[hh].d])
                            for sub in range(4):
                                for e_ in range(2):
                                    tk = sub * 128 + e_ * 64
                                    P.mm(pA[e_ * 64:(e_ + 1) * 64, hf * 256 + sub * 64: hf * 256 + (sub + 1) * 64],
                                         hkm[hh][:, tk:tk + 64], hqm[hh][:, tk:tk + 64], True, True,
                                         [hkm[hh].d, hqm[hh].d], [pA.d])
                            P.tt("dve", am[hh][:], pA[:, hf * 256:(hf + 1) * 256], hmask[:], ALU.mult,
                                 [pA.d, hmask.d], [am[hh].d])
                        for c in range(8):
                            sub, e_ = c // 2, c % 2
                            tk = sub * 128 + e_ * 64
                            rows = slice(e_ * 64, (e_ + 1) * 64)
                            for hh in heads:
                                slot = c % 4
                                pS = pSb[hh]
                                pSs = pS[:, slot * 128:(slot + 1) * 128]
                                vch = hv[rows, sub, hh * 128:(hh + 1) * 128]
                                po = pO[hh]
                                P.mm(po[:, c * 64:(c + 1) * 64], Tbf[hh][:], hqs[hh][:, tk:tk + 64], True, False,
                                     [Tbf[hh].d, hqs[hh].d], [po.d])
                                P.mm(po[:, c * 64:(c + 1) * 64], vch, am[hh][rows, sub * 64:(sub + 1) * 64], False, True,
                                     [hv.d, am[hh].d], [po.d])
                                P.mm(pSs, hketm[hh][rows, sub, :], vch, True, True, [hketm[hh].d, hv.d], [pS.d])
                                P.stt("dve", Tst[hh][:], Tst[hh][:], dch[hh][:, c:c + 1], pSs, ALU.mult, ALU.add,
                                      [pS.d, dch[hh].d], [Tst[hh].d])
                                P.cp("dve", Tbf[hh][:], Tst[hh][:], [Tst[hh].d], [Tbf[hh].d])
                        for hh in heads:
                            po = pO[hh]
                            P.cp("dve", osb[:], po[:, :], [po.d], [osb.d])
                            P.act(sq[:], po[:, :], AF.Square, [po.d], [sq.d])
                            pn = pb[2]
                            P.mm(pn[:, :], onesm[:], sq[:], True, True, [onesm.d, sq.d], [pn.d])
                            P.act(rno[:], pn[:, :], AF.Sqrt, [pn.d], [rno.d], bias=eps_t[:, 0:1])
                            P.op("dve", lambda e: e.reciprocal(out=rno[:], in_=rno[:]), [rno.d], [rno.d])
                            P.tt("pool", osb[:], osb[:], rno[:], ALU.mult, [rno.d], [osb.d])
                            P.tt("pool", mst[:, hh, :], osb[:], gate[hh][:], ALU.mult, [osb.d, gate[hh].d], [mst.d])
                    P.dma("sp", mix_d[4:8, :, t0:t0 + TT].rearrange("m p t -> p m t"), mst[:], [mst.d], [], dsem(mst))
                P.barrier()
                P.flush()
                P.recycle()

            with contextlib.ExitStack() as eb:
                q2 = sb(eb, "q2", [128, S], BF16)
                k2 = sb(eb, "k2", [128, S], BF16)
                qp = [q2] + [sb(eb, f"qp{i}", [128, S], BF16) for i in (1, 2)]
                kp = [k2] + [sb(eb, f"kp{i}", [128, S], BF16) for i in (1, 2)]
                vt = [[sb(eb, f"vt{j}_{i}", [128, 32, 128], BF16) for i in range(3)] for j in range(2)]
                acc = [sb(eb, f"acc{j}", [128, S], F32) for j in range(2)]
                PT = [sb(eb, f"PT{i}", [128, 256], BF16) for i in range(3)]
                lsh = sb(eb, "lsh", [64, S], F32)
                ost = [sb(eb, f"ost{j}", [64, S], BF16) for j in range(2)]
                wst = [sb(eb, f"wst{i}", [128, 1024], F32) for i in range(3)]
                wsb = [sb(eb, f"wsb{i}", [128, 1024], BF16) for i in range(3)]

                def prep_gen():
                    i = 0
                    for kc in range(8):
                        for q in range(4):
                            a, b_ = wst[i % 3], wsb[i % 3]
                            load(a, a[:], w_up[l, kc * 128:(kc + 1) * 128, q * 1024:(q + 1) * 1024], eng="sp")
                            cast("pool", b_[:], a[:], nmlp[:, l, kc:kc + 1], [a.d], [b_.d])
                            P.dma("sp", wup_d[:, kc, q * 1024:(q + 1) * 1024], b_[:], [b_.d], [], dsem(b_))
                            i += 1
                            yield
                    for hc in range(32):
                        a, b_ = wst[i % 3], wsb[i % 3]
                        load(a, a[:], w_down[l, hc * 128:(hc + 1) * 128, :], eng="sp")
                        cast("pool", b_[:], a[:], None, [a.d], [b_.d])
                        P.dma("sp", wdn_d[:, hc, :], b_[:], [b_.d], [], dsem(b_))
                        i += 1
                        yield

                pg = prep_gen()
                for j in range(2):
                    for i in range(3):
                        P.memset("pool", vt[j][i][:, :, 64:128], 1.0, [vt[j][i].d])
                psr = [pb[0], pb[1], pb[2]]
                por = [pb[3], pb[4], pb[5]]
                blk_i = 0
                for m in range(4):
                    load(q2, q2[:], qr_d[m])
                    load(k2, k2[:], kr_d[m], eng="pool")
                    for pi_, dl in enumerate(PATTERNS):
                        if dl == 1:
                            continue
                        P.cp("pool", qp[pi_][:], q2[:].rearrange("p (m r) -> p r m", r=dl), [q2.d], [qp[pi_].d])
                        P.cp("pool", kp[pi_][:], k2[:].rearrange("p (m r) -> p r m", r=dl), [k2.d], [kp[pi_].d])
                    for e_ in range(2):
                        h = 2 * m + e_
                        rows = slice(e_ * 64, (e_ + 1) * 64)
                        ac = acc[e_]
                        for pi_, dl in enumerate(PATTERNS):
                            v_ = vt[e_][pi_]
                            vsrc = va_d[:, h * 64:(h + 1) * 64].rearrange("(n j dl) c -> dl j n c", j=128, dl=dl)
                            nblk = 32 // dl
                            for r in range(dl):
                                P.dma("sp", v_[:, r * nblk:(r + 1) * nblk, 0:64], vsrc[r], [],
                                      [v_.d], dsem(v_))
                        blocks = []
                        for pi_, dl in enumerate(PATTERNS):
                            nblk = 32 // dl
                            acv = ac[:].rearrange("p (m r) -> p r m", r=dl)
                            for r in range(dl):
                                for n in range(nblk):
                                    blocks.append((pi_, dl, nblk, r, n, acv))

                        def emit_scores(i):
                            pi_, dl, nblk, r, n, acv = blocks[i]
                            qq, kk_ = qp[pi_], kp[pi_]
                            bq = r * nblk + n
                            nq = 256 if n < nblk - 1 else 128
                            pS_ = psr[i % 3]
                            pt_ = PT[i % 3]
                            P.mm(pS_[:, 0:nq], kk_[rows, bq * 128:(bq + 1) * 128],
                                 qq[rows, bq * 128:bq * 128 + nq], True, False, [kk_.d, qq.d], [pS_.d])
                            P.mm(pS_[:, 0:nq], ident[:], mbias[:, 0:nq], False, True, [ident.d, mbias.d],
                                 [pS_.d])
                            P.act(pt_[:, 0:nq], pS_[:, 0:nq], AF.Exp, [pS_.d], [pt_.d])

                        def emit_pv(i):
                            pi_, dl, nblk, r, n, acv = blocks[i]
                            v_ = vt[e_][pi_]
                            bq = r * nblk + n
                            pt_ = PT[i % 3]
                            g, sl = divmod(bq, 4)
                            po = por[g % 3]
                            P.mm(po[:, sl * 128:(sl + 1) * 128], v_[:, bq, :], pt_[:, 0:128], n == 0, True,
                                 [v_.d, pt_.d], [po.d])
                            if n < nblk - 1:
                                g2, sl2 = divmod(bq + 1, 4)
                                po2 = por[g2 % 3]
                                P.mm(po2[:, sl2 * 128:(sl2 + 1) * 128], v_[:, bq, :], pt_[:, 128:256], True,
                                     False, [v_.d, pt_.d], [po2.d])
                            if (sl == 3) or (n == nblk - 1):
                                first = max(g * 4, r * nblk)
                                cnt = (bq - first + 1) * 128
                                n_first = first - r * nblk
                                dstv = acv[:, r, n_first * 128: n_first * 128 + cnt]
                                srcv = po[:, (first - g * 4) * 128:(first - g * 4) * 128 + cnt]
                                if pi_ == 0:
                                    P.cp("dve", dstv, srcv, [po.d], [ac.d])
                                else:
                                    P.tt("dve", dstv, dstv, srcv, ALU.add, [po.d], [ac.d])

                        LOOK = 2
                        for i in range(len(blocks) + LOOK):
                            if i < len(blocks):
                                emit_scores(i)
                            if i >= LOOK:
                                emit_pv(i - LOOK)
                            blk_i += 1
                            if blk_i % 6 == 0:
                                next(pg, None)
                        P.cp("dve", lsh[:], ac[64:128, :], [ac.d], [lsh.d])
                        P.op("dve", lambda e: e.reciprocal(out=lsh[:], in_=lsh[:]), [lsh.d], [lsh.d])
                        P.tt("pool", ost[e_][:], ac[0:64, :], lsh[:], ALU.mult, [ac.d, lsh.d], [ost[e_].d])
                        P.dma("sp", mix_d[m, e_ * 64:(e_ + 1) * 64, :], ost[e_][:], [ost[e_].d], [], dsem(ost[e_]))
                for _ in pg:
                    pass
                P.barrier()
                P.flush()
                P.recycle()

            with contextlib.ExitStack() as ec:
                Wo = sb(ec, "Wo", [128, 8, D], BF16)
                stg = [sb(ec, f"stgC{i}", [128, 1024], F32) for i in range(2)]
                xt = sb(ec, "xtC", [128, 4, D], F32)
                mt = sb(ec, "mt", [128, 8, TT], BF16)
                sqa = sb(ec, "sqa", [128, 4, TT], BF16)
                rsa = sb(ec, "rsa", [128, 4], F32)
                tmp = [sb(ec, f"tmpC{i}", [128, 512], F32) for i in range(2)]
                xn = sb(ec, "xnC", [128, 4, D], BF16, nd=4)
                hT = sb(ec, "hTC", [128, 8, TT], BF16, nd=8)
                junk = sb(ec, "junkC", [128, D], BF16)
                ss = sb(ec, "ssC", [128, 4], F32)
                rt = sb(ec, "rtC", [128, 4], F32)
                rs = sb(ec, "rsC", [128, 4], F32)
                uT = sb(ec, "uT", [128, 32, TT], BF16, nd=32)
                rl = [sb(ec, f"rl{i}", [128, TT], BF16) for i in range(2)]
                wu = [sb(ec, f"wu{i}", [128, 8, 512], BF16) for i in range(3)]
                wd = [sb(ec, f"wd{i}", [128, 4, 512], BF16) for i in range(4)]
                xo = sb(ec, "xo", [128, 4, D], F32)
                nfin = sb(ec, "nfin", [128, D], F32)
                last = (l == NL - 1)
                if last:
                    load(nfin, nfin[:], p_nfin)
                for kc in range(8):
                    st = stg[kc % 2]
                    load(st, st[:], w_out[l, kc * 128:(kc + 1) * 128, :], eng=("sp", "pool")[kc % 2])
                    gsc = again[:, l, kc:kc + 1] if kc < 4 else hgain[:, l, 0:1]
                    cast(cast_eng(), Wo[:, kc, :], st[:], gsc, [st.d], [Wo.d])
                gen_i = [0]

                def gbank():
                    gen_i[0] += 1
                    return pb[gen_i[0] % 3]

                for tt in range(NT):
                    t0 = tt * TT
                    load(xt, xt[:], x_src[t0:t0 + TT, :].rearrange("(s p) d -> p s d", p=128))
                    load(mt, mt[:], mix_d[:, :, t0:t0 + TT].rearrange("m p t -> p m t"), eng="pool")
                    P.act(sqa[:], mt[:, 0:4, :], AF.Square, [mt.d], [sqa.d])
                    pss = pb[6]
                    for sub in range(4):
                        for kc in range(4):
                            P.mm(pss[:, sub:sub + 1], sqa[:, kc, sub * 128:(sub + 1) * 128], onec[:], kc == 0, kc == 3,
                                 [sqa.d, onec.d], [pss.d])
                    P.act(rsa[:], pss[:, 0:4], AF.Sqrt, [pss.d], [rsa.d], scale=1.0 / 512.0, bias=eps_t[:, 0:1])
                    P.op("dve", lambda e: e.reciprocal(out=rsa[:], in_=rsa[:]), [rsa.d], [rsa.d])
                    for sub in range(4):
                        for nh in range(2):
                            pa = gbank()
                            for kc in range(4):
                                P.mm(pa[:, :], mt[:, kc, sub * 128:(sub + 1) * 128], Wo[:, kc, nh * 512:(nh + 1) * 512],
                                     kc == 0, kc == 3, [mt.d, Wo.d], [pa.d])
                            ph = gbank()
                            for kc in range(4, 8):
                                P.mm(ph[:, :], mt[:, kc, sub * 128:(sub + 1) * 128], Wo[:, kc, nh * 512:(nh + 1) * 512],
                                     kc == 4, kc == 7, [mt.d, Wo.d], [ph.d])
                            xs_ = xt[:, sub, nh * 512:(nh + 1) * 512]
                            tm_ = tmp[nh]
                            P.stt("dve", tm_[:], pa[:, :], rsa[:, sub:sub + 1], xs_, ALU.mult, ALU.add,
                                  [pa.d, rsa.d, xt.d], [tm_.d])
                            P.tt("dve", xs_, tm_[:], ph[:, :], ALU.add, [tm_.d, ph.d], [xt.d])
                    P.memset("dve", ss[:], 0.0, [ss.d])
                    for sub in range(4):
                        P.act(junk[:], xt[:, sub, :], AF.Square, [xt.d], [junk.d, ss.d], accum=ss[:, sub:sub + 1])
                    P.act(rt[:], ss[:], AF.Sqrt, [ss.d], [rt.d], scale=1.0 / D, bias=eps_t[:, 0:1])
                    P.op("dve", lambda e: e.reciprocal(out=rs[:], in_=rt[:]), [rt.d], [rs.d])
                    for sub in range(4):
                        eng = ("pool", "dve", "pool", "act")[sub]
                        cast(eng, xn[:, sub, :], xt[:, sub, :], rs[:, sub:sub + 1], [xt.d, rs.d], [xn.ds[sub]])
                    for kc in range(8):
                        hf = kc % 2
                        for sub in range(4):
                            P.tr(ptb[:, hf * 512 + sub * 128: hf * 512 + (sub + 1) * 128],
                                 xn[:, sub, kc * 128:(kc + 1) * 128], ident[:], [xn.ds[sub], ident.d], [ptb.ds[hf]])
                        P.cp(("act", "dve")[kc % 2], hT[:, kc, :], ptb[:, hf * 512:(hf + 1) * 512], [ptb.ds[hf]],
                             [hT.ds[kc]])
                    for g in range(8):
                        w_ = wu[(tt * 8 + g) % 3]
                        load(w_, w_[:], wup_d[:, :, g * 512:(g + 1) * 512], eng="sp")
                        for hl in range(4):
                            hc = g * 4 + hl
                            pu = gbank()
                            for kc in range(8):
                                P.mm(pu[:, :], w_[:, kc, hl * 128:(hl + 1) * 128], hT[:, kc, :], kc == 0, kc == 7,
                                     [w_.d, hT.ds[kc]], [pu.d])
                            r_ = rl[hc % 2]
                            P.act(r_[:], pu[:, :], AF.Relu, [pu.d], [r_.d])
                            P.tt(("pool", "dve")[hc % 2], uT[:, hc, :], r_[:], r_[:], ALU.mult, [r_.d], [uT.ds[hc]])
                    for nh in range(2):
                        py = [pb[3], pb[4], pb[5], pb[6]]
                        for g in range(8):
                            wi_ = (nh * 8 + g) % 4
                            w_ = wd[wi_]
                            load(w_, w_[:], wdn_d[:, g * 4:(g + 1) * 4, nh * 512:(nh + 1) * 512], eng="sp")
                            for sub in range(4):
                                for hl in range(4):
                                    hc = g * 4 + hl
                                    P.mm(py[sub][:, :], uT[:, hc, sub * 128:(sub + 1) * 128], w_[:, hl, :], hc == 0,
                                         hc == 31, [uT.ds[hc], w_.d], [py[sub].d])
                        for sub in range(4):
                            P.tt("dve", xo[:, sub, nh * 512:(nh + 1) * 512], xt[:, sub, nh * 512:(nh + 1) * 512],
                                 py[sub][:, :], ALU.add, [xt.d, py[sub].d], [xo.d])
                    if not last:
                        P.dma("sp", xres[t0:t0 + TT, :].rearrange("(s p) d -> p s d", p=128), xo[:], [xo.d], [],
                              dsem(xo))
                    else:
                        P.memset("dve", ss[:], 0.0, [ss.d])
                        for sub in range(4):
                            P.act(junk[:], xo[:, sub, :], AF.Square, [xo.d], [junk.d, ss.d], accum=ss[:, sub:sub + 1])
                        P.act(rt[:], ss[:], AF.Sqrt, [ss.d], [rt.d], scale=1.0 / D, bias=eps_t[:, 0:1])
                        P.op("dve", lambda e: e.reciprocal(out=rs[:], in_=rt[:]), [rt.d], [rs.d])
                        for sub in range(4):
                            P.stt("dve", xo[:, sub, :], xo[:, sub, :], rs[:, sub:sub + 1], nfin[:],
                                  ALU.mult, ALU.mult, [rs.d, nfin.d], [xo.d])
                        P.dma("sp", y_out[t0:t0 + TT, :].rearrange("(s p) d -> p s d", p=128), xo[:], [xo.d], [],
                              dsem(xo))
                P.barrier()
                P.flush()
                P.recycle()
    return nc


def _consts():
    f32 = np.float32
    half = 32
    inv = (10000.0 ** (-np.arange(half, dtype=f32) / half)).astype(f32)
    pos = np.arange(S, dtype=f32)
    ang = (pos[:, None] * inv[None, :]).astype(f32)
    cos, sin = np.cos(ang).astype(f32), np.sin(ang).astype(f32)
    fidx = (np.arange(128) % 64) % 32
    c_cos = np.ascontiguousarray(cos[:, fidx].T)
    c_sin = np.ascontiguousarray(sin[:, fidx].T)
    rot = np.zeros((128, 128), f32)
    for m in range(128):
        if (m % 64) < 32:
            rot[m + 32, m] = -1.0
        else:
            rot[m - 32, m] = 1.0
    ident = np.eye(128, dtype=f32)
    j = np.arange(128)[:, None]
    i = np.arange(128)[None, :]
    mb = np.concatenate([np.where(j <= i, 0.0, NEG), np.where(j >= i, 0.0, NEG)], axis=1).astype(f32)
    jj = (np.arange(128) % 64)[:, None]
    ii = np.arange(64)[None, :]
    hm = np.tile((jj <= ii).astype(f32), (1, 4))
    return dict(c_cos=c_cos, c_sin=c_sin, c_rot=rot, c_ident=ident, c_mbias=mb, c_hmask=hm)


_CACHE = {}


def kernel(x, norm_mix, w_in, attn_out_gain, hgrn_lb_logits, hgrn_out_gain, w_out, norm_mlp, w_up, w_down,
           norm_final):
    f32 = np.float32
    a = lambda v: np.ascontiguousarray(np.asarray(v, dtype=f32))
    if "nc" not in _CACHE:
        _CACHE["nc"] = build()
    nc = _CACHE["nc"]
    col = lambda v, k: np.ascontiguousarray(a(v).reshape(NL, k, 128).transpose(2, 0, 1))
    shared = dict(
        w_in=a(w_in), w_out=a(w_out), w_up=a(w_up), w_down=a(w_down),
        p_nmix=col(norm_mix, 8), p_again=col(attn_out_gain, 4), p_lbl=col(hgrn_lb_logits, 4),
        p_hgain=col(hgrn_out_gain, 1), p_nmlp=col(norm_mlp, 8),
        p_nfin=np.ascontiguousarray(np.broadcast_to(a(norm_final)[None, :], (128, D))),
    )
    shared.update(_consts())
    xs = a(x)
    in_maps = [dict(shared, x=np.ascontiguousarray(xs[b])) for b in range(8)]
    res = run_bass_kernel_spmd(nc, in_maps, core_ids=list(range(8)))
    _CACHE["res"] = res
    return np.stack([r["y"] for r in res.results], axis=0).astype(f32)
```

```python
import contextlib
import numpy as np
import concourse.bass as bass
import concourse.mybir as mybir
from concourse.bass_utils import run_bass_kernel_spmd

F32 = mybir.dt.float32
BF16 = mybir.dt.bfloat16
AF = mybir.ActivationFunctionType
ALU = mybir.AluOpType

S, D, NL = 4096, 1024, 2
TT = 512
NT = S // TT
INW = 3584
HID = 4096
EPS = 1e-6
NEG = -30000.0
PATTERNS = (1, 4, 16)
DEBUG = False
SAME_SYNC = True


class Sem:
    def __init__(self, h):
        self.h = h
        self.c = 0


class Tl:
    __slots__ = ("w", "r", "name", "small")

    def __init__(self, name="", small=False):
        self.w = None
        self.r = {}
        self.name = name
        self.small = small


class Buf:
    def __init__(self, t, name, nd=1, small=False):
        self.t = t
        self.d = Tl(name, small)
        self.ds = [Tl(f"{name}{i}", small) for i in range(nd)] if nd > 1 else None
        self.sem = None

    def __getitem__(self, k):
        return self.t[k]


class Prog:
    ENG = ["sp", "act", "pe", "dve", "pool"]

    def __init__(self, nc, es):
        self.nc = nc
        self.es = es
        self.stream = {e: [] for e in self.ENG}
        self.dsems = []
        self.free_sems = {}
        self.phase_sems = []
        self.esem = {e: self.newsem("e_" + e, False) for e in self.ENG}
        self.seen = {e: {} for e in self.ENG}
        self.nins = 0

    def newsem(self, name, dma=True, q="sp"):
        if dma and self.free_sems.get(q):
            s = self.free_sems[q].pop()
        else:
            s = Sem(self.es.enter_context(self.nc.semaphore(name)))
            if dma:
                self.dsems.append(s)
        if dma:
            self.phase_sems.append((q, s))
        return s

    def recycle(self):
        for q, s in self.phase_sems:
            self.free_sems.setdefault(q, []).append(s)
        self.phase_sems = []

    def _wait(self, eng, sem, val):
        if self.seen[eng].get(sem, 0) >= val:
            return
        self.seen[eng][sem] = val
        h = sem.h
        self.stream[eng].append(lambda e: e.wait_ge(h, val))

    def _sync(self, eng, reads, writes, skip_own=True):
        own = self.esem[eng] if (skip_own and (eng == "pe" or not SAME_SYNC)) else None
        for t in reads:
            if t.w is not None and (t.w[0] is not own or (t.small and eng != "pe")):
                self._wait(eng, *t.w)
        for t in writes:
            if t.w is not None and (t.w[0] is not own or (t.small and eng != "pe")):
                self._wait(eng, *t.w)
            for s, v in t.r.items():
                if s is not own:
                    self._wait(eng, s, v)

    def op(self, eng, fn, reads=(), writes=()):
        self._sync(eng, reads, writes)
        s = self.esem[eng]
        s.c += 1
        v = s.c
        h = s.h
        self.stream[eng].append(lambda e: fn(e).then_inc(h, 1))
        self.nins += 1
        for t in reads:
            t.r[s] = v
        for t in writes:
            t.w = (s, v)
            t.r = {}

    def dma(self, eng, out, in_, reads, writes, dsem):
        self._sync(eng, reads, writes, skip_own=False)
        dsem.c += 16
        v = dsem.c
        h = dsem.h
        self.stream[eng].append(lambda e: e.dma_start(out=out, in_=in_).then_inc(h, 16))
        self.nins += 1
        for t in reads:
            t.r[dsem] = v
        for t in writes:
            t.w = (dsem, v)
            t.r = {}

    def barrier(self):
        for e in self.ENG:
            for e2 in self.ENG:
                if e2 != e:
                    self._wait(e, self.esem[e2], self.esem[e2].c)
            for ds in self.dsems:
                self._wait(e, ds, ds.c)

    def flush(self):
        st = self.stream
        with self.nc.Block() as block:
            @block.sync
            def _(e):
                for f in st["sp"]:
                    f(e)

            @block.scalar
            def _(e):
                for f in st["act"]:
                    f(e)

            @block.tensor
            def _(e):
                for f in st["pe"]:
                    f(e)

            @block.vector
            def _(e):
                for f in st["dve"]:
                    f(e)

            @block.gpsimd
            def _(e):
                for f in st["pool"]:
                    f(e)
        self.stream = {e: [] for e in self.ENG}

    def mm(self, out, lhsT, rhs, start, stop, reads, writes):
        self.op("pe", lambda e: e.matmul(out, lhsT=lhsT, rhs=rhs, start=start, stop=stop), reads, writes)

    def tr(self, out, in_, ident, reads, writes):
        self.op("pe", lambda e: e.transpose(out, in_, ident), reads, writes)

    def act(self, out, in_, func, reads, writes, scale=None, bias=None, accum=None):
        kw = {}
        if scale is not None:
            kw["scale"] = scale
        if bias is not None:
            kw["bias"] = bias
        if accum is not None:
            kw["accum_out"] = accum
        self.op("act", lambda e: e.activation(out=out, in_=in_, func=func, **kw), reads, writes)

    def tt(self, eng, out, in0, in1, op, reads, writes):
        self.op(eng, lambda e: e.tensor_tensor(out=out, in0=in0, in1=in1, op=op), reads, writes)

    def ts(self, eng, out, in0, s1, s2, op0, op1, reads, writes):
        if s2 is None:
            self.op(eng, lambda e: e.tensor_scalar(out=out, in0=in0, scalar1=s1, scalar2=None, op0=op0), reads, writes)
        else:
            self.op(eng, lambda e: e.tensor_scalar(out=out, in0=in0, scalar1=s1, scalar2=s2, op0=op0, op1=op1), reads, writes)

    def stt(self, eng, out, in0, scalar, in1, op0, op1, reads, writes):
        self.op(eng, lambda e: e.scalar_tensor_tensor(out=out, in0=in0, scalar=scalar, in1=in1, op0=op0, op1=op1), reads, writes)

    def cp(self, eng, out, in_, reads, writes):
        if eng == "act":
            self.op("act", lambda e: e.activation(out=out, in_=in_, func=AF.Copy), reads, writes)
        else:
            self.op(eng, lambda e: e.tensor_copy(out=out, in_=in_), reads, writes)

    def memset(self, eng, ap, val, writes):
        self.op(eng, lambda e: e.memset(ap, val), (), writes)


def build():
    nc = bass.Bass("TRN2", target_bir_lowering=False)
    dt_in = lambda n, s: nc.dram_tensor(n, s, F32, kind="ExternalInput").ap()
    x_in = dt_in("x", [S, D])
    w_in = dt_in("w_in", [NL, D, INW])
    w_out = dt_in("w_out", [NL, D, D])
    w_up = dt_in("w_up", [NL, D, HID])
    w_down = dt_in("w_down", [NL, HID, D])
    p_nmix = dt_in("p_nmix", [128, NL, 8])
    p_again = dt_in("p_again", [128, NL, 4])
    p_lbl = dt_in("p_lbl", [128, NL, 4])
    p_hgain = dt_in("p_hgain", [128, NL, 1])
    p_nmlp = dt_in("p_nmlp", [128, NL, 8])
    p_nfin = dt_in("p_nfin", [128, D])
    c_cos = dt_in("c_cos", [128, S])
    c_sin = dt_in("c_sin", [128, S])
    c_rot = dt_in("c_rot", [128, 128])
    c_ident = dt_in("c_ident", [128, 128])
    c_mbias = dt_in("c_mbias", [128, 256])
    c_hmask = dt_in("c_hmask", [128, 256])
    y_out = nc.dram_tensor("y", [S, D], F32, kind="ExternalOutput").ap()

    kind_s = "ExternalOutput" if DEBUG else "Internal"
    scr = lambda n, s, d: nc.dram_tensor(n, s, d, kind=kind_s).ap()
    xres = scr("xres", [S, D], F32)
    qr_d = scr("qr_d", [4, 128, S], BF16)
    kr_d = scr("kr_d", [4, 128, S], BF16)
    va_d = scr("va_d", [S, 512], BF16)
    mix_d = scr("mix_d", [8, 128, S], BF16)
    wup_d = scr("wup_d", [128, 8, HID], BF16)
    wdn_d = scr("wdn_d", [128, 32, D], BF16)

    with contextlib.ExitStack() as es:
        P = Prog(nc, es)

        sfx = [""]

        def sb(es_, name, shape, dt, nd=1):
            name = name + sfx[0]
            small = int(np.prod(shape[1:])) <= 256
            return Buf(es_.enter_context(nc.sbuf_tensor(name, shape, dt)), name, nd, small)

        def ps(es_, name, shape, dt, nd=1):
            return Buf(es_.enter_context(nc.psum_tensor(name, shape, dt)), name, nd)

        def dsem(buf, q="sp"):
            if buf.sem is None:
                buf.sem = P.newsem("d_" + buf.d.name, True, q)
                buf.semq = q
            assert buf.semq == q, (buf.d.name, buf.semq, q)
            return buf.sem

        def load(buf, dst_ap, src_ap, eng="sp", tile=None):
            P.dma(eng, dst_ap, src_ap, [], [tile or buf.d], dsem(buf, eng))

        ident = sb(es, "ident", [128, 128], BF16)
        rot = sb(es, "rot", [128, 128], BF16)
        mbias = sb(es, "mbias", [128, 256], BF16)
        hmask = sb(es, "hmask", [128, 256], F32)
        onesm = sb(es, "onesm", [128, 128], BF16)
        onec = sb(es, "onec", [128, 1], BF16)
        ones_f = sb(es, "ones_f", [128, 512], F32)
        eps_t = sb(es, "eps_t", [128, 1], F32)
        nmix = sb(es, "nmix", [128, NL, 8], F32)
        again = sb(es, "again", [128, NL, 4], F32)
        hgain = sb(es, "hgain", [128, NL, 1], F32)
        nmlp = sb(es, "nmlp", [128, NL, 8], F32)
        lbl = sb(es, "lbl", [128, NL, 4], F32)
        lb = sb(es, "lb", [128, NL, 4], F32)
        oml = sb(es, "oml", [128, NL, 4], F32)
        noml = sb(es, "noml", [128, NL, 4], F32)
        nomlc = sb(es, "nomlc", [128, NL, 4], F32)
        omlc = sb(es, "omlc", [128, NL, 4], F32)
        cst = sb(es, "cst", [128, 256], F32)
        Tst = [sb(es, f"Tst{h}", [128, 128], F32) for h in range(4)]
        Tbf = [sb(es, f"Tbf{h}", [128, 128], BF16) for h in range(4)]

        pb = [ps(es, f"pb{i}", [128, 512], F32) for i in range(7)]
        ptb = ps(es, "ptb", [128, 1024], BF16, nd=2)
        ptb.ds = [ptb.d, ptb.d]

        for i, (src, dstb) in enumerate([(c_ident, ident), (c_rot, rot)]):
            load(cst, cst[:, 0:128], src)
            P.cp("dve", dstb[:], cst[:, 0:128], [cst.d], [dstb.d])
        load(cst, cst[:, :], c_mbias)
        P.cp("dve", mbias[:], cst[:, :], [cst.d], [mbias.d])
        load(hmask, hmask[:], c_hmask)
        P.memset("pool", onesm[:], 1.0 / 128.0, [onesm.d])
        P.memset("pool", onec[:], 1.0, [onec.d])
        P.memset("pool", ones_f[:], 1.0, [ones_f.d])
        P.memset("pool", eps_t[:], EPS, [eps_t.d])
        for (src, dstb) in [(p_nmix, nmix), (p_again, again), (p_hgain, hgain), (p_nmlp, nmlp), (p_lbl, lbl)]:
            load(dstb, dstb[:], src)
        P.memset("dve", lb[:, 0, :], 0.0, [lb.d])
        P.tt("dve", lb[:, 1, :], lbl[:, 1, :], lbl[:, 0, :], ALU.subtract, [lbl.d], [lb.d])
        P.act(lb[:, 1, :], lb[:, 1, :], AF.Sigmoid, [lb.d], [lb.d])
        P.ts("dve", oml[:], lb[:], -1.0, 1.0, ALU.mult, ALU.add, [lb.d], [oml.d])
        P.ts("dve", noml[:], oml[:], -1.0, None, ALU.mult, None, [oml.d], [noml.d])
        P.ts("dve", omlc[:], oml[:], float(128 ** -0.5), None, ALU.mult, None, [oml.d], [omlc.d])
        P.ts("dve", nomlc[:], oml[:], -float(128 ** -0.5), None, ALU.mult, None, [oml.d], [nomlc.d])
        P.barrier()
        P.flush()

        rr_state = {"pmm": 0, "cast": 0}

        def cast_eng():
            rr_state["cast"] += 1
            return ("act", "dve", "pool")[rr_state["cast"] % 3]

        def cast(eng, out, in_, scale, reads, writes):
            if scale is None:
                P.cp(eng, out, in_, reads, writes)
            elif eng == "act":
                P.act(out, in_, AF.Copy, reads, writes, scale=scale)
            else:
                P.ts(eng, out, in_, scale, None, ALU.mult, None, reads, writes)

        for l in range(NL):
            x_src = x_in if l == 0 else xres
            sfx[0] = f"_L{l}"
            with contextlib.ExitStack() as ea:
                Win = sb(ea, "Win", [128, 8, INW], BF16)
                stg = [sb(ea, f"stgA{i}", [128, 1792], F32) for i in range(2)]
                xt = sb(ea, "xt", [128, 4, D], F32)
                xn = sb(ea, "xn", [128, 4, D], BF16, nd=4)
                hT = sb(ea, "hT", [128, 8, TT], BF16, nd=8)
                junk = sb(ea, "junk", [128, D], BF16)
                ss = sb(ea, "ss", [128, 4], F32)
                rt = sb(ea, "rt", [128, 4], F32)
                rs = sb(ea, "rs", [128, 4], F32)
                cs = sb(ea, "cs", [128, TT], F32)
                sn = sb(ea, "sn", [128, TT], F32)
                qsb = [sb(ea, f"qsb{i}", [128, TT], BF16) for i in range(2)]
                t1 = [sb(ea, f"t1_{i}", [128, TT], F32) for i in range(2)]
                t2 = [sb(ea, f"t2_{i}", [128, TT], F32) for i in range(2)]
                qst = sb(ea, "qst", [128, 4, TT], BF16)
                kst = sb(ea, "kst", [128, 4, TT], BF16)
                vst = sb(ea, "vst", [128, 4, 512], BF16)
                hv = sb(ea, "hv", [128, 4, 512], BF16)
                sil = sb(ea, "sil", [128, TT], F32)
                sg = sb(ea, "sg", [128, TT], F32)
                lf = sb(ea, "lf", [128, TT], F32)
                kk = sb(ea, "kk", [128, TT], F32)
                Lc = sb(ea, "Lc", [128, 8, 64], F32)
                Lst = sb(ea, "Lst", [128, 8], F32)
                Dd = [sb(ea, f"Dd{i}", [128, 8, 64], F32) for i in range(2)]
                Ee = [sb(ea, f"Ee{i}", [128, 8, 64], F32) for i in range(2)]
                dtmp = sb(ea, "dtmp", [128, 8], F32)
                dch = [sb(ea, f"dch{h}", [128, 8], F32) for h in range(4)]
                hqm = [sb(ea, f"hqm{h}", [128, TT], BF16) for h in range(4)]
                hqs = [sb(ea, f"hqs{h}", [128, TT], BF16) for h in range(4)]
                hkm = [sb(ea, f"hkm{h}", [128, TT], BF16) for h in range(4)]
                hke = [sb(ea, f"hke{h}", [128, TT], BF16) for h in range(4)]
                gate = [sb(ea, f"gate{h}", [128, TT], BF16) for h in range(4)]
                hketm = [sb(ea, f"hketm{h}", [128, 4, 128], BF16) for h in range(4)]
                am = [sb(ea, f"am{h}", [128, 256], BF16) for h in range(4)]
                osb = sb(ea, "osb", [128, TT], F32)
                sq = sb(ea, "sq", [128, TT], BF16)
                rno = sb(ea, "rno", [128, TT], F32)
                mst = sb(ea, "mst", [128, 4, TT], BF16)

                pi = 0
                for kc in range(8):
                    for hf in range(2):
                        st = stg[pi % 2]
                        load(st, st[:], w_in[l, kc * 128:(kc + 1) * 128, hf * 1792:(hf + 1) * 1792],
                             eng=("sp", "pool")[pi % 2])
                        cast(cast_eng(), Win[:, kc, hf * 1792:(hf + 1) * 1792], st[:], nmix[:, l, kc:kc + 1],
                             [st.d], [Win.d])
                        pi += 1
                for h in range(4):
                    P.memset("pool", Tst[h][:], 0.0, [Tst[h].d])
                    P.memset("pool", Tbf[h][:], 0.0, [Tbf[h].d])
                P.memset("pool", Lst[:, 0:1], 0.0, [Lst.d])

                pmm_i = [0]

                def next_pmm():
                    pmm_i[0] += 1
                    return pb[pmm_i[0] % 2]

                def fm_chunk(col0):
                    b = next_pmm()
                    for kc in range(8):
                        P.mm(b[:, :], Win[:, kc, col0:col0 + 128], hT[:, kc, :], kc == 0, kc == 7,
                             [Win.d, hT.ds[kc]], [b.d])
                    return b

                def tm_sub(col0, sub):
                    b = next_pmm()
                    for kc in range(8):
                        P.mm(b[:, :], hT[:, kc, sub * 128:(sub + 1) * 128], Win[:, kc, col0:col0 + 512], kc == 0,
                             kc == 7, [Win.d, hT.ds[kc]], [b.d])
                    return b

                c_hq = float(128 ** -0.5)
                for tt in range(NT):
                    t0 = tt * TT
                    load(xt, xt[:], x_src[t0:t0 + TT, :].rearrange("(s p) d -> p s d", p=128))
                    load(cs, cs[:], c_cos[:, t0:t0 + TT], eng="pool")
                    load(sn, sn[:], c_sin[:, t0:t0 + TT], eng="pool")
                    P.memset("dve", ss[:], 0.0, [ss.d])
                    for sub in range(4):
                        P.act(junk[:], xt[:, sub, :], AF.Square, [xt.d], [junk.d, ss.d], accum=ss[:, sub:sub + 1])
                    P.act(rt[:], ss[:], AF.Sqrt, [ss.d], [rt.d], scale=1.0 / D, bias=eps_t[:, 0:1])
                    P.op("dve", lambda e: e.reciprocal(out=rs[:], in_=rt[:]), [rt.d], [rs.d])
                    for sub in range(4):
                        eng = ("act", "dve", "dve", "act")[sub]
                        cast(eng, xn[:, sub, :], xt[:, sub, :], rs[:, sub:sub + 1], [xt.d, rs.d], [xn.ds[sub]])
                    for kc in range(8):
                        hf = kc % 2
                        for sub in range(4):
                            P.tr(ptb[:, hf * 512 + sub * 128: hf * 512 + (sub + 1) * 128],
                                 xn[:, sub, kc * 128:(kc + 1) * 128], ident[:], [xn.ds[sub], ident.d], [ptb.ds[hf]])
                        P.cp(("act", "dve")[kc % 2], hT[:, kc, :], ptb[:, hf * 512:(hf + 1) * 512], [ptb.ds[hf]],
                             [hT.ds[kc]])

                    for hh in range(4):
                        b = fm_chunk(2048 + hh * 128)
                        P.act(sg[:], b[:, :], AF.Sigmoid, [b.d], [sg.d])
                        P.act(lf[:], sg[:], AF.Ln, [sg.d], [lf.d], scale=oml[:, l, hh:hh + 1], bias=lb[:, l, hh:hh + 1])
                        P.ts("dve", kk[:], sg[:], nomlc[:, l, hh:hh + 1], omlc[:, l, hh:hh + 1], ALU.mult, ALU.add,
                             [sg.d], [kk.d])
                        Lflat = Lc[:].rearrange("p c i -> p (c i)")
                        P.op("dve", lambda e, Lflat=Lflat: e.tensor_tensor_scan(out=Lflat, data0=ones_f[:], data1=lf[:],
                                                                                 initial=0.0, op0=ALU.mult, op1=ALU.add),
                             [ones_f.d, lf.d], [Lc.d])
                        P.cp("dve", Lst[:, 1:8], Lc[:, 0:7, 63], [Lc.d], [Lst.d])
                        P.tt("dve", dtmp[:], Lc[:, :, 63], Lst[:], ALU.subtract, [Lc.d, Lst.d], [dtmp.d])
                        P.act(dch[hh][:], dtmp[:], AF.Exp, [dtmp.d], [dch[hh].d])
                        b = fm_chunk(1536 + hh * 128)
                        P.act(sil[:], b[:, :], AF.Silu, [b.d], [sil.d])
                        bc = lambda ap: ap.broadcast_to([128, 8, 64])
                        sil3 = sil[:].rearrange("p (c i) -> p c i", i=64)
                        kk3 = kk[:].rearrange("p (c i) -> p c i", i=64)
                        v3 = lambda bf: bf[:].rearrange("p (c i) -> p c i", i=64)
                        D0, D1, E0, E1 = Dd[0], Dd[1], Ee[0], Ee[1]
                        P.tt("dve", D0[:], Lc[:], bc(Lc[:, :, 31:32]), ALU.subtract, [Lc.d], [D0.d])
                        P.act(E0[:], D0[:], AF.Exp, [D0.d], [E0.d])
                        P.act(E1[:], D0[:], AF.Exp, [D0.d], [E1.d], scale=-1.0)
                        P.tt("dve", v3(hqm[hh]), sil3, E0[:], ALU.mult, [sil.d, E0.d], [hqm[hh].d])
                        P.tt("pool", v3(hkm[hh]), kk3, E1[:], ALU.mult, [kk.d, E1.d], [hkm[hh].d])
                        P.tt("dve", D1[:], Lc[:], bc(Lst[:].rearrange("p (c o) -> p c o", o=1)), ALU.subtract,
                             [Lc.d, Lst.d], [D1.d])
                        P.act(E0[:], D1[:], AF.Exp, [D1.d], [E0.d])
                        P.tt("pool", v3(hqs[hh]), sil3, E0[:], ALU.mult, [sil.d, E0.d], [hqs[hh].d])
                        P.tt("dve", D0[:], bc(Lc[:, :, 63:64]), Lc[:], ALU.subtract, [Lc.d], [D0.d])
                        P.act(E1[:], D0[:], AF.Exp, [D0.d], [E1.d])
                        P.tt("dve", v3(hke[hh]), kk3, E1[:], ALU.mult, [kk.d, E1.d], [hke[hh].d])
                        b = fm_chunk(3072 + hh * 128)
                        P.act(gate[hh][:], b[:, :], AF.Silu, [b.d], [gate[hh].d])
                    for sub in range(4):
                        b = tm_sub(2560, sub)
                        P.cp(("act", "dve")[sub % 2], hv[:, sub, :], b[:, :], [b.d], [hv.d])

                    for which, (col0, stt_, scl) in enumerate([(0, qst, 0.125), (512, kst, None)]):
                        for m in range(4):
                            b = fm_chunk(col0 + m * 128)
                            qs_ = qsb[m % 2]
                            if scl is None:
                                P.cp("act", qs_[:], b[:, :], [b.d], [qs_.d])
                            else:
                                P.act(qs_[:], b[:, :], AF.Copy, [b.d], [qs_.d], scale=scl)
                            pr = pb[2]
                            P.mm(pr[:, :], rot[:], qs_[:], True, True, [rot.d, qs_.d], [pr.d])
                            a1, a2 = t1[m % 2], t2[m % 2]
                            P.tt("dve", a2[:], pr[:, :], sn[:], ALU.mult, [pr.d, sn.d], [a2.d])
                            P.tt("pool", a1[:], qs_[:], cs[:], ALU.mult, [qs_.d, cs.d], [a1.d])
                            P.tt("dve", stt_[:, m, :], a1[:], a2[:], ALU.add, [a1.d, a2.d], [stt_.d])
                        dst = (qr_d, kr_d)[which]
                        P.dma("sp", dst[:, :, t0:t0 + TT].rearrange("m p t -> p m t"), stt_[:], [stt_.d], [],
                              dsem(stt_))
                    for sub in range(4):
                        b = tm_sub(1024, sub)
                        P.cp(("act", "dve")[sub % 2], vst[:, sub, :], b[:, :], [b.d], [vst.d])
                    P.dma("sp", va_d[t0:t0 + TT, :].rearrange("(s p) c -> p s c", p=128), vst[:], [vst.d], [],
                          dsem(vst))

                    for hp in range(2):
                        heads = (2 * hp, 2 * hp + 1)
                        pA = pb[3]
                        pSb = {heads[0]: pb[3], heads[1]: pb[4]}
                        pO = {heads[0]: pb[5], heads[1]: pb[6]}
                        for hh in heads:
                            hf = hh % 2
                            for sub in range(4):
                                P.tr(ptb[:, hf * 512 + sub * 128: hf * 512 + (sub + 1) * 128],
                                     hke[hh][:, sub * 128:(sub + 1) * 128], ident[:], [hke[hh].d, ident.d],
                                     [ptb.ds[hf]])
                            P.cp("act", hketm[hh][:].rearrange("p s f -> p (s f)"), ptb[:, hf * 512:(hf + 1) * 512],
                                 [ptb.ds[hf]], [hketm[hh].d])
                            for sub in range(4):
                                for e_ in range(2):
                                    tk = sub * 128 + e_ * 64
                                    P.mm(pA[e_ * 64:(e_ + 1) * 64, hf * 256 + sub * 64: hf * 256 + (sub + 1) * 64],
                                         hkm[hh][:, tk:tk + 64], hqm[hh][:, tk:tk + 64], True, True,
                                         [hkm[hh].d, hqm[hh].d], [pA.d])
                            P.tt("dve", am[hh][:], pA[:, hf * 256:(hf + 1) * 256], hmask[:], ALU.mult,
                                 [pA.d, hmask.d], [am[hh].d])
                        for c in range(8):
                            sub, e_ = c // 2, c % 2
                            tk = sub * 128 + e_ * 64
                            rows = slice(e_ * 64, (e_ + 1) * 64)
                            for hh in heads:
                                slot = c % 4
                                pS = pSb[hh]
                                pSs = pS[:, slot * 128:(slot + 1) * 128]
                                vch = hv[rows, sub, hh * 128:(hh + 1) * 128]
                                po = pO[hh]
                                P.mm(po[:, c * 64:(c + 1) * 64], Tbf[hh][:], hqs[hh][:, tk:tk + 64], True, False,
                                     [Tbf[hh].d, hqs[hh].d], [po.d])
                                P.mm(po[:, c * 64:(c + 1) * 64], vch, am[hh][rows, sub * 64:(sub + 1) * 64], False, True,
                                     [hv.d, am[hh].d], [po.d])
                                P.mm(pSs, hketm[hh][rows, sub, :], vch, True, True, [hketm[hh].d, hv.d], [pS.d])
                                P.stt("dve", Tst[hh][:], Tst[hh][:], dch[hh][:, c:c + 1], pSs, ALU.mult, ALU.add,
                                      [pS.d, dch[hh].d], [Tst[hh].d])
                                P.cp("dve", Tbf[hh][:], Tst[hh][:], [Tst[hh].d], [Tbf[hh].d])
                        for hh in heads:
                            po = pO[hh]
                            P.cp("dve", osb[:], po[:, :], [po.d], [osb.d])
                            P.act(sq[:], po[:, :], AF.Square, [po.d], [sq.d])
                            pn = pb[2]
                            P.mm(pn[:, :], onesm[:], sq[:], True, True, [onesm.d, sq.d], [pn.d])
                            P.act(rno[:], pn[:, :], AF.Sqrt, [pn.d], [rno.d], bias=eps_t[:, 0:1])
                            P.op("dve", lambda e: e.reciprocal(out=rno[:], in_=rno[:]), [rno.d], [rno.d])
                            P.tt("dve", osb[:], osb[:], rno[:], ALU.mult, [rno.d], [osb.d])
                            P.tt("pool", mst[:, hh, :], osb[:], gate[hh][:], ALU.mult, [osb.d, gate[hh].d], [mst.d])
                    P.dma("sp", mix_d[4:8, :, t0:t0 + TT].rearrange("m p t -> p m t"), mst[:], [mst.d], [], dsem(mst))
                P.barrier()
                P.flush()
                P.recycle()

            with contextlib.ExitStack() as eb:
                q2 = sb(eb, "q2", [128, S], BF16)
                k2 = sb(eb, "k2", [128, S], BF16)
                qp = [q2] + [sb(eb, f"qp{i}", [128, S], BF16) for i in (1, 2)]
                kp = [k2] + [sb(eb, f"kp{i}", [128, S], BF16) for i in (1, 2)]
                vt = [[sb(eb, f"vt{j}_{i}", [128, 32, 128], BF16) for i in range(3)] for j in range(2)]
                acc = [sb(eb, f"acc{j}", [128, S], F32) for j in range(2)]
                PT = [sb(eb, f"PT{i}", [128, 256], BF16) for i in range(3)]
                lsh = sb(eb, "lsh", [64, S], F32)
                ost = [sb(eb, f"ost{j}", [64, S], BF16) for j in range(2)]
                wst = [sb(eb, f"wst{i}", [128, 1024], F32) for i in range(3)]
                wsb = [sb(eb, f"wsb{i}", [128, 1024], BF16) for i in range(3)]

                def prep_gen():
                    i = 0
                    for kc in range(8):
                        for q in range(4):
                            a, b_ = wst[i % 3], wsb[i % 3]
                            load(a, a[:], w_up[l, kc * 128:(kc + 1) * 128, q * 1024:(q + 1) * 1024], eng="sp")
                            cast("pool", b_[:], a[:], nmlp[:, l, kc:kc + 1], [a.d], [b_.d])
                            P.dma("sp", wup_d[:, kc, q * 1024:(q + 1) * 1024], b_[:], [b_.d], [], dsem(b_))
                            i += 1
                            yield
                    for hc in range(32):
                        a, b_ = wst[i % 3], wsb[i % 3]
                        load(a, a[:], w_down[l, hc * 128:(hc + 1) * 128, :], eng="sp")
                        cast("pool", b_[:], a[:], None, [a.d], [b_.d])
                        P.dma("sp", wdn_d[:, hc, :], b_[:], [b_.d], [], dsem(b_))
                        i += 1
                        yield

                pg = prep_gen()
                for j in range(2):
                    for i in range(3):
                        P.memset("pool", vt[j][i][:, :, 64:128], 1.0, [vt[j][i].d])
                psr = [pb[0], pb[1], pb[2]]
                por = [pb[3], pb[4], pb[5]]
                blk_i = 0
                for m in range(4):
                    load(q2, q2[:], qr_d[m])
                    load(k2, k2[:], kr_d[m], eng="pool")
                    for pi_, dl in enumerate(PATTERNS):
                        if dl == 1:
                            continue
                        P.cp("dve", qp[pi_][:], q2[:].rearrange("p (m r) -> p r m", r=dl), [q2.d], [qp[pi_].d])
                        P.cp("act", kp[pi_][:], k2[:].rearrange("p (m r) -> p r m", r=dl), [k2.d], [kp[pi_].d])
                    for e_ in range(2):
                        h = 2 * m + e_
                        rows = slice(e_ * 64, (e_ + 1) * 64)
                        ac = acc[e_]
                        for pi_, dl in enumerate(PATTERNS):
                            v_ = vt[e_][pi_]
                            vsrc = va_d[:, h * 64:(h + 1) * 64].rearrange("(n j dl) c -> dl j n c", j=128, dl=dl)
                            nblk = 32 // dl
                            for r in range(dl):
                                P.dma("sp", v_[:, r * nblk:(r + 1) * nblk, 0:64], vsrc[r], [],
                                      [v_.d], dsem(v_))
                        blocks = []
                        for pi_, dl in enumerate(PATTERNS):
                            nblk = 32 // dl
                            acv = ac[:].rearrange("p (m r) -> p r m", r=dl)
                            for r in range(dl):
                                for n in range(nblk):
                                    blocks.append((pi_, dl, nblk, r, n, acv))

                        def emit_scores(i):
                            pi_, dl, nblk, r, n, acv = blocks[i]
                            qq, kk_ = qp[pi_], kp[pi_]
                            bq = r * nblk + n
                            nq = 256 if n < nblk - 1 else 128
                            pS_ = psr[i % 3]
                            pt_ = PT[i % 3]
                            P.mm(pS_[:, 0:nq], kk_[rows, bq * 128:(bq + 1) * 128],
                                 qq[rows, bq * 128:bq * 128 + nq], True, False, [kk_.d, qq.d], [pS_.d])
                            P.mm(pS_[:, 0:nq], ident[:], mbias[:, 0:nq], False, True, [ident.d, mbias.d],
                                 [pS_.d])
                            P.act(pt_[:, 0:nq], pS_[:, 0:nq], AF.Exp, [pS_.d], [pt_.d])

                        def emit_pv(i):
                            pi_, dl, nblk, r, n, acv = blocks[i]
                            v_ = vt[e_][pi_]
                            bq = r * nblk + n
                            pt_ = PT[i % 3]
                            g, sl = divmod(bq, 4)
                            po = por[g % 3]
                            P.mm(po[:, sl * 128:(sl + 1) * 128], v_[:, bq, :], pt_[:, 0:128], n == 0, True,
                                 [v_.d, pt_.d], [po.d])
                            if n < nblk - 1:
                                g2, sl2 = divmod(bq + 1, 4)
                                po2 = por[g2 % 3]
                                P.mm(po2[:, sl2 * 128:(sl2 + 1) * 128], v_[:, bq, :], pt_[:, 128:256], True,
                                     False, [v_.d, pt_.d], [po2.d])
                            if (sl == 3) or (n == nblk - 1):
                                first = max(g * 4, r * nblk)
                                cnt = (bq - first + 1) * 128
                                n_first = first - r * nblk
                                dstv = acv[:, r, n_first * 128: n_first * 128 + cnt]
                                srcv = po[:, (first - g * 4) * 128:(first - g * 4) * 128 + cnt]
                                if pi_ == 0:
                                    P.cp("dve", dstv, srcv, [po.d], [ac.d])
                                else:
                                    P.tt("dve", dstv, dstv, srcv, ALU.add, [po.d], [ac.d])

                        LOOK = 2
                        for i in range(len(blocks) + LOOK):
                            if i < len(blocks):
                                emit_scores(i)
                            if i >= LOOK:
                                emit_pv(i - LOOK)
                            blk_i += 1
                            if blk_i % 6 == 0:
                                next(pg, None)
                        P.cp("dve", lsh[:], ac[64:128, :], [ac.d], [lsh.d])
                        P.op("dve", lambda e: e.reciprocal(out=lsh[:], in_=lsh[:]), [lsh.d], [lsh.d])
                        P.tt("pool", ost[e_][:], ac[0:64, :], lsh[:], ALU.mult, [ac.d, lsh.d], [ost[e_].d])
                        P.dma("sp", mix_d[m, e_ * 64:(e_ + 1) * 64, :], ost[e_][:], [ost[e_].d], [], dsem(ost[e_]))
                for _ in pg:
                    pass
                P.barrier()
                P.flush()
                P.recycle()

            with contextlib.ExitStack() as ec:
                Wo = sb(ec, "Wo", [128, 8, D], BF16)
                stg = [sb(ec, f"stgC{i}", [128, 1024], F32) for i in range(2)]
                xt = sb(ec, "xtC", [128, 4, D], F32)
                mt = sb(ec, "mt", [128, 8, TT], BF16)
                sqa = sb(ec, "sqa", [128, 4, TT], BF16)
                rsa = sb(ec, "rsa", [128, 4], F32)
                tmp = [sb(ec, f"tmpC{i}", [128, 512], F32) for i in range(2)]
                xn = sb(ec, "xnC", [128, 4, D], BF16, nd=4)
                hT = sb(ec, "hTC", [128, 8, TT], BF16, nd=8)
                junk = sb(ec, "junkC", [128, D], BF16)
                ss = sb(ec, "ssC", [128, 4], F32)
                rt = sb(ec, "rtC", [128, 4], F32)
                rs = sb(ec, "rsC", [128, 4], F32)
                uT = sb(ec, "uT", [128, 32, TT], BF16, nd=32)
                rl = [sb(ec, f"rl{i}", [128, TT], BF16) for i in range(2)]
                wu = [sb(ec, f"wu{i}", [128, 8, 512], BF16) for i in range(3)]
                wd = [sb(ec, f"wd{i}", [128, 4, 512], BF16) for i in range(4)]
                xo = sb(ec, "xo", [128, 4, D], F32)
                nfin = sb(ec, "nfin", [128, D], F32)
                last = (l == NL - 1)
                if last:
                    load(nfin, nfin[:], p_nfin)
                for kc in range(8):
                    st = stg[kc % 2]
                    load(st, st[:], w_out[l, kc * 128:(kc + 1) * 128, :], eng=("sp", "pool")[kc % 2])
                    gsc = again[:, l, kc:kc + 1] if kc < 4 else hgain[:, l, 0:1]
                    cast(cast_eng(), Wo[:, kc, :], st[:], gsc, [st.d], [Wo.d])
                gen_i = [0]

                def gbank():
                    gen_i[0] += 1
                    return pb[gen_i[0] % 3]

                for tt in range(NT):
                    t0 = tt * TT
                    load(xt, xt[:], x_src[t0:t0 + TT, :].rearrange("(s p) d -> p s d", p=128))
                    load(mt, mt[:], mix_d[:, :, t0:t0 + TT].rearrange("m p t -> p m t"), eng="pool")
                    P.act(sqa[:], mt[:, 0:4, :], AF.Square, [mt.d], [sqa.d])
                    pss = pb[6]
                    for sub in range(4):
                        for kc in range(4):
                            P.mm(pss[:, sub:sub + 1], sqa[:, kc, sub * 128:(sub + 1) * 128], onec[:], kc == 0, kc == 3,
                                 [sqa.d, onec.d], [pss.d])
                    P.act(rsa[:], pss[:, 0:4], AF.Sqrt, [pss.d], [rsa.d], scale=1.0 / 512.0, bias=eps_t[:, 0:1])
                    P.op("dve", lambda e: e.reciprocal(out=rsa[:], in_=rsa[:]), [rsa.d], [rsa.d])
                    for sub in range(4):
                        for nh in range(2):
                            pa = gbank()
                            for kc in range(4):
                                P.mm(pa[:, :], mt[:, kc, sub * 128:(sub + 1) * 128], Wo[:, kc, nh * 512:(nh + 1) * 512],
                                     kc == 0, kc == 3, [mt.d, Wo.d], [pa.d])
                            ph = gbank()
                            for kc in range(4, 8):
                                P.mm(ph[:, :], mt[:, kc, sub * 128:(sub + 1) * 128], Wo[:, kc, nh * 512:(nh + 1) * 512],
                                     kc == 4, kc == 7, [mt.d, Wo.d], [ph.d])
                            xs_ = xt[:, sub, nh * 512:(nh + 1) * 512]
                            tm_ = tmp[nh]
                            P.stt("dve", tm_[:], pa[:, :], rsa[:, sub:sub + 1], xs_, ALU.mult, ALU.add,
                                  [pa.d, rsa.d, xt.d], [tm_.d])
                            P.tt("dve", xs_, tm_[:], ph[:, :], ALU.add, [tm_.d, ph.d], [xt.d])
                    P.memset("dve", ss[:], 0.0, [ss.d])
                    for sub in range(4):
                        P.act(junk[:], xt[:, sub, :], AF.Square, [xt.d], [junk.d, ss.d], accum=ss[:, sub:sub + 1])
                    P.act(rt[:], ss[:], AF.Sqrt, [ss.d], [rt.d], scale=1.0 / D, bias=eps_t[:, 0:1])
                    P.op("dve", lambda e: e.reciprocal(out=rs[:], in_=rt[:]), [rt.d], [rs.d])
                    for sub in range(4):
                        eng = ("act", "dve", "dve", "act")[sub]
                        cast(eng, xn[:, sub, :], xt[:, sub, :], rs[:, sub:sub + 1], [xt.d, rs.d], [xn.ds[sub]])
                    for kc in range(8):
                        hf = kc % 2
                        for sub in range(4):
                            P.tr(ptb[:, hf * 512 + sub * 128: hf * 512 + (sub + 1) * 128],
                                 xn[:, sub, kc * 128:(kc + 1) * 128], ident[:], [xn.ds[sub], ident.d], [ptb.ds[hf]])
                        P.cp(("act", "dve")[kc % 2], hT[:, kc, :], ptb[:, hf * 512:(hf + 1) * 512], [ptb.ds[hf]],
                             [hT.ds[kc]])
                    for g in range(8):
                        w_ = wu[(tt * 8 + g) % 3]
                        load(w_, w_[:], wup_d[:, :, g * 512:(g + 1) * 512], eng="sp")
                        for hl in range(4):
                            hc = g * 4 + hl
                            pu = gbank()
                            for kc in range(8):
                                P.mm(pu[:, :], w_[:, kc, hl * 128:(hl + 1) * 128], hT[:, kc, :], kc == 0, kc == 7,
                                     [w_.d, hT.ds[kc]], [pu.d])
                            r_ = rl[hc % 2]
                            P.act(r_[:], pu[:, :], AF.Relu, [pu.d], [r_.d])
                            P.tt(("pool", "dve")[hc % 2], uT[:, hc, :], r_[:], r_[:], ALU.mult, [r_.d], [uT.ds[hc]])
                    for nh in range(2):
                        py = [pb[3], pb[4], pb[5], pb[6]]
                        for g in range(8):
                            wi_ = (nh * 8 + g) % 4
                            w_ = wd[wi_]
                            load(w_, w_[:], wdn_d[:, g * 4:(g + 1) * 4, nh * 512:(nh + 1) * 512], eng="sp")
                            for sub in range(4):
                                for hl in range(4):
                                    hc = g * 4 + hl
                                    P.mm(py[sub][:, :], uT[:, hc, sub * 128:(sub + 1) * 128], w_[:, hl, :], hc == 0,
                                         hc == 31, [uT.ds[hc], w_.d], [py[sub].d])
                        for sub in range(4):
                            P.tt("dve", xo[:, sub, nh * 512:(nh + 1) * 512], xt[:, sub, nh * 512:(nh + 1) * 512],
                                 py[sub][:, :], ALU.add, [xt.d, py[sub].d], [xo.d])
                    if not last:
                        P.dma("sp", xres[t0:t0 + TT, :].rearrange("(s p) d -> p s d", p=128), xo[:], [xo.d], [],
                              dsem(xo))
                    else:
                        P.memset("dve", ss[:], 0.0, [ss.d])
                        for sub in range(4):
                            P.act(junk[:], xo[:, sub, :], AF.Square, [xo.d], [junk.d, ss.d], accum=ss[:, sub:sub + 1])
                        P.act(rt[:], ss[:], AF.Sqrt, [ss.d], [rt.d], scale=1.0 / D, bias=eps_t[:, 0:1])
                        P.op("dve", lambda e: e.reciprocal(out=rs[:], in_=rt[:]), [rt.d], [rs.d])
                        for sub in range(4):
                            P.stt("dve", xo[:, sub, :], xo[:, sub, :], rs[:, sub:sub + 1], nfin[:],
                                  ALU.mult, ALU.mult, [rs.d, nfin.d], [xo.d])
                        P.dma("sp", y_out[t0:t0 + TT, :].rearrange("(s p) d -> p s d", p=128), xo[:], [xo.d], [],
                              dsem(xo))
                P.barrier()
                P.flush()
                P.recycle()
    return nc


def _consts():
    f32 = np.float32
    half = 32
    inv = (10000.0 ** (-np.arange(half, dtype=f32) / half)).astype(f32)
    pos = np.arange(S, dtype=f32)
    ang = (pos[:, None] * inv[None, :]).astype(f32)
    cos, sin = np.cos(ang).astype(f32), np.sin(ang).astype(f32)
    fidx = (np.arange(128) % 64) % 32
    c_cos = np.ascontiguousarray(cos[:, fidx].T)
    c_sin = np.ascontiguousarray(sin[:, fidx].T)
    rot = np.zeros((128, 128), f32)
    for m in range(128):
        if (m % 64) < 32:
            rot[m + 32, m] = -1.0
        else:
            rot[m - 32, m] = 1.0
    ident = np.eye(128, dtype=f32)
    j = np.arange(128)[:, None]
    i = np.arange(128)[None, :]
    mb = np.concatenate([np.where(j <= i, 0.0, NEG), np.where(j >= i, 0.0, NEG)], axis=1).astype(f32)
    jj = (np.arange(128) % 64)[:, None]
    ii = np.arange(64)[None, :]
    hm = np.tile((jj <= ii).astype(f32), (1, 4))
    return dict(c_cos=c_cos, c_sin=c_sin, c_rot=rot, c_ident=ident, c_mbias=mb, c_hmask=hm)


_CACHE = {}


def kernel(x, norm_mix, w_in, attn_out_gain, hgrn_lb_logits, hgrn_out_gain, w_out, norm_mlp, w_up, w_down,
           norm_final):
    f32 = np.float32
    a = lambda v: np.ascontiguousarray(np.asarray(v, dtype=f32))
    if "nc" not in _CACHE:
        _CACHE["nc"] = build()
    nc = _CACHE["nc"]
    col = lambda v, k: np.ascontiguousarray(a(v).reshape(NL, k, 128).transpose(2, 0, 1))
    shared = dict(
        w_in=a(w_in), w_out=a(w_out), w_up=a(w_up), w_down=a(w_down),
        p_nmix=col(norm_mix, 8), p_again=col(attn_out_gain, 4), p_lbl=col(hgrn_lb_logits, 4),
        p_hgain=col(hgrn_out_gain, 1), p_nmlp=col(norm_mlp, 8),
        p_nfin=np.ascontiguousarray(np.broadcast_to(a(norm_final)[None, :], (128, D))),
    )
    shared.update(_consts())
    xs = a(x)
    in_maps = [dict(shared, x=np.ascontiguousarray(xs[b])) for b in range(8)]
    res = run_bass_kernel_spmd(nc, in_maps, core_ids=list(range(8)))
    _CACHE["res"] = res
    return np.stack([r["y"] for r in res.results], axis=0).astype(f32)
```

```python
import contextlib
import numpy as np
import concourse.bass as bass
import concourse.mybir as mybir
from concourse.bass_utils import run_bass_kernel_spmd

F32 = mybir.dt.float32
BF16 = mybir.dt.bfloat16
AF = mybir.ActivationFunctionType
ALU = mybir.AluOpType

S, D, NL = 4096, 1024, 2
TT = 512
NT = S // TT
INW = 3584
HID = 4096
EPS = 1e-6
NEG = -30000.0
PATTERNS = (1, 4, 16)
DEBUG = False
SAME_SYNC = True


class Sem:
    def __init__(self, h):
        self.h = h
        self.c = 0


class Tl:
    __slots__ = ("w", "r", "name", "small")

    def __init__(self, name="", small=False):
        self.w = None
        self.r = {}
        self.name = name
        self.small = small


class Buf:
    def __init__(self, t, name, nd=1, small=False):
        self.t = t
        self.d = Tl(name, small)
        self.ds = [Tl(f"{name}{i}", small) for i in range(nd)] if nd > 1 else None
        self.sem = None

    def __getitem__(self, k):
        return self.t[k]


class Prog:
    ENG = ["sp", "act", "pe", "dve", "pool"]

    def __init__(self, nc, es):
        self.nc = nc
        self.es = es
        self.stream = {e: [] for e in self.ENG}
        self.dsems = []
        self.free_sems = {}
        self.phase_sems = []
        self.esem = {e: self.newsem("e_" + e, False) for e in self.ENG}
        self.seen = {e: {} for e in self.ENG}
        self.nins = 0

    def newsem(self, name, dma=True, q="sp"):
        if dma and self.free_sems.get(q):
            s = self.free_sems[q].pop()
        else:
            s = Sem(self.es.enter_context(self.nc.semaphore(name)))
            if dma:
                self.dsems.append(s)
        if dma:
            self.phase_sems.append((q, s))
        return s

    def recycle(self):
        for q, s in self.phase_sems:
            self.free_sems.setdefault(q, []).append(s)
        self.phase_sems = []

    def _wait(self, eng, sem, val):
        if self.seen[eng].get(sem, 0) >= val:
            return
        self.seen[eng][sem] = val
        h = sem.h
        self.stream[eng].append(lambda e: e.wait_ge(h, val))

    def _sync(self, eng, reads, writes, skip_own=True):
        own = self.esem[eng] if (skip_own and (eng == "pe" or not SAME_SYNC)) else None
        for t in reads:
            if t.w is not None and (t.w[0] is not own or (t.small and eng != "pe")):
                self._wait(eng, *t.w)
        for t in writes:
            if t.w is not None and (t.w[0] is not own or (t.small and eng != "pe")):
                self._wait(eng, *t.w)
            for s, v in t.r.items():
                if s is not own:
                    self._wait(eng, s, v)

    def op(self, eng, fn, reads=(), writes=()):
        self._sync(eng, reads, writes)
        s = self.esem[eng]
        s.c += 1
        v = s.c
        h = s.h
        self.stream[eng].append(lambda e: fn(e).then_inc(h, 1))
        self.nins += 1
        for t in reads:
            t.r[s] = v
        for t in writes:
            t.w = (s, v)
            t.r = {}

    def dma(self, eng, out, in_, reads, writes, dsem):
        self._sync(eng, reads, writes, skip_own=False)
        dsem.c += 16
        v = dsem.c
        h = dsem.h
        self.stream[eng].append(lambda e: e.dma_start(out=out, in_=in_).then_inc(h, 16))
        self.nins += 1
        for t in reads:
            t.r[dsem] = v
        for t in writes:
            t.w = (dsem, v)
            t.r = {}

    def barrier(self):
        for e in self.ENG:
            for e2 in self.ENG:
                if e2 != e:
                    self._wait(e, self.esem[e2], self.esem[e2].c)
            for ds in self.dsems:
                self._wait(e, ds, ds.c)

    def flush(self):
        st = self.stream
        with self.nc.Block() as block:
            @block.sync
            def _(e):
                for f in st["sp"]:
                    f(e)

            @block.scalar
            def _(e):
                for f in st["act"]:
                    f(e)

            @block.tensor
            def _(e):
                for f in st["pe"]:
                    f(e)

            @block.vector
            def _(e):
                for f in st["dve"]:
                    f(e)

            @block.gpsimd
            def _(e):
                for f in st["pool"]:
                    f(e)
        self.stream = {e: [] for e in self.ENG}

    def mm(self, out, lhsT, rhs, start, stop, reads, writes):
        self.op("pe", lambda e: e.matmul(out, lhsT=lhsT, rhs=rhs, start=start, stop=stop), reads, writes)

    def tr(self, out, in_, ident, reads, writes):
        self.op("pe", lambda e: e.transpose(out, in_, ident), reads, writes)

    def act(self, out, in_, func, reads, writes, scale=None, bias=None, accum=None):
        kw = {}
        if scale is not None:
            kw["scale"] = scale
        if bias is not None:
            kw["bias"] = bias
        if accum is not None:
            kw["accum_out"] = accum
        self.op("act", lambda e: e.activation(out=out, in_=in_, func=func, **kw), reads, writes)

    def tt(self, eng, out, in0, in1, op, reads, writes):
        self.op(eng, lambda e: e.tensor_tensor(out=out, in0=in0, in1=in1, op=op), reads, writes)

    def ts(self, eng, out, in0, s1, s2, op0, op1, reads, writes):
        if s2 is None:
            self.op(eng, lambda e: e.tensor_scalar(out=out, in0=in0, scalar1=s1, scalar2=None, op0=op0), reads, writes)
        else:
            self.op(eng, lambda e: e.tensor_scalar(out=out, in0=in0, scalar1=s1, scalar2=s2, op0=op0, op1=op1), reads, writes)

    def stt(self, eng, out, in0, scalar, in1, op0, op1, reads, writes):
        self.op(eng, lambda e: e.scalar_tensor_tensor(out=out, in0=in0, scalar=scalar, in1=in1, op0=op0, op1=op1), reads, writes)

    def cp(self, eng, out, in_, reads, writes):
        if eng == "act":
            self.op("act", lambda e: e.activation(out=out, in_=in_, func=AF.Copy), reads, writes)
        else:
            self.op(eng, lambda e: e.tensor_copy(out=out, in_=in_), reads, writes)

    def memset(self, eng, ap, val, writes):
        self.op(eng, lambda e: e.memset(ap, val), (), writes)


def build():
    nc = bass.Bass("TRN2", target_bir_lowering=False)
    dt_in = lambda n, s: nc.dram_tensor(n, s, F32, kind="ExternalInput").ap()
    x_in = dt_in("x", [S, D])
    w_in = dt_in("w_in", [NL, D, INW])
    w_out = dt_in("w_out", [NL, D, D])
    w_up = dt_in("w_up", [NL, D, HID])
    w_down = dt_in("w_down", [NL, HID, D])
    p_nmix = dt_in("p_nmix", [128, NL, 8])
    p_again = dt_in("p_again", [128, NL, 4])
    p_lbl = dt_in("p_lbl", [128, NL, 4])
    p_hgain = dt_in("p_hgain", [128, NL, 1])
    p_nmlp = dt_in("p_nmlp", [128, NL, 8])
    p_nfin = dt_in("p_nfin", [128, D])
    c_cos = dt_in("c_cos", [128, S])
    c_sin = dt_in("c_sin", [128, S])
    c_rot = dt_in("c_rot", [128, 128])
    c_ident = dt_in("c_ident", [128, 128])
    c_mbias = dt_in("c_mbias", [128, 256])
    c_hmask = dt_in("c_hmask", [128, 256])
    y_out = nc.dram_tensor("y", [S, D], F32, kind="ExternalOutput").ap()

    kind_s = "ExternalOutput" if DEBUG else "Internal"
    scr = lambda n, s, d: nc.dram_tensor(n, s, d, kind=kind_s).ap()
    xres = scr("xres", [S, D], F32)
    qr_d = scr("qr_d", [4, 128, S], BF16)
    kr_d = scr("kr_d", [4, 128, S], BF16)
    va_d = scr("va_d", [S, 512], BF16)
    mix_d = scr("mix_d", [8, 128, S], BF16)
    wup_d = scr("wup_d", [128, 8, HID], BF16)
    wdn_d = scr("wdn_d", [128, 32, D], BF16)

    with contextlib.ExitStack() as es:
        P = Prog(nc, es)

        sfx = [""]

        def sb(es_, name, shape, dt, nd=1):
            name = name + sfx[0]
            small = int(np.prod(shape[1:])) <= 256
            return Buf(es_.enter_context(nc.sbuf_tensor(name, shape, dt)), name, nd, small)

        def ps(es_, name, shape, dt, nd=1):
            return Buf(es_.enter_context(nc.psum_tensor(name, shape, dt)), name, nd)

        def dsem(buf, q="sp"):
            if buf.sem is None:
                buf.sem = P.newsem("d_" + buf.d.name, True, q)
                buf.semq = q
            assert buf.semq == q, (buf.d.name, buf.semq, q)
            return buf.sem

        def load(buf, dst_ap, src_ap, eng="sp", tile=None):
            P.dma(eng, dst_ap, src_ap, [], [tile or buf.d], dsem(buf, eng))

        ident = sb(es, "ident", [128, 128], BF16)
        rot = sb(es, "rot", [128, 128], BF16)
        mbias = sb(es, "mbias", [128, 256], BF16)
        hmask = sb(es, "hmask", [128, 256], F32)
        onesm = sb(es, "onesm", [128, 128], BF16)
        onec = sb(es, "onec", [128, 1], BF16)
        ones_f = sb(es, "ones_f", [128, 512], F32)
        eps_t = sb(es, "eps_t", [128, 1], F32)
        nmix = sb(es, "nmix", [128, NL, 8], F32)
        again = sb(es, "again", [128, NL, 4], F32)
        hgain = sb(es, "hgain", [128, NL, 1], F32)
        nmlp = sb(es, "nmlp", [128, NL, 8], F32)
        lbl = sb(es, "lbl", [128, NL, 4], F32)
        lb = sb(es, "lb", [128, NL, 4], F32)
        oml = sb(es, "oml", [128, NL, 4], F32)
        noml = sb(es, "noml", [128, NL, 4], F32)
        nomlc = sb(es, "nomlc", [128, NL, 4], F32)
        omlc = sb(es, "omlc", [128, NL, 4], F32)
        cst = sb(es, "cst", [128, 256], F32)
        Tst = [sb(es, f"Tst{h}", [128, 128], F32) for h in range(4)]
        Tbf = [sb(es, f"Tbf{h}", [128, 128], BF16) for h in range(4)]

        pb = [ps(es, f"pb{i}", [128, 512], F32) for i in range(7)]
        ptb = ps(es, "ptb", [128, 1024], BF16, nd=2)
        ptb.ds = [ptb.d, ptb.d]

        for i, (src, dstb) in enumerate([(c_ident, ident), (c_rot, rot)]):
            load(cst, cst[:, 0:128], src)
            P.cp("dve", dstb[:], cst[:, 0:128], [cst.d], [dstb.d])
        load(cst, cst[:, :], c_mbias)
        P.cp("dve", mbias[:], cst[:, :], [cst.d], [mbias.d])
        load(hmask, hmask[:], c_hmask)
        P.memset("pool", onesm[:], 1.0 / 128.0, [onesm.d])
        P.memset("pool", onec[:], 1.0, [onec.d])
        P.memset("pool", ones_f[:], 1.0, [ones_f.d])
        P.memset("pool", eps_t[:], EPS, [eps_t.d])
        for (src, dstb) in [(p_nmix, nmix), (p_again, again), (p_hgain, hgain), (p_nmlp, nmlp), (p_lbl, lbl)]:
            load(dstb, dstb[:], src)
        P.memset("dve", lb[:, 0, :], 0.0, [lb.d])
        P.tt("dve", lb[:, 1, :], lbl[:, 1, :], lbl[:, 0, :], ALU.subtract, [lbl.d], [lb.d])
        P.act(lb[:, 1, :], lb[:, 1, :], AF.Sigmoid, [lb.d], [lb.d])
        P.ts("dve", oml[:], lb[:], -1.0, 1.0, ALU.mult, ALU.add, [lb.d], [oml.d])
        P.ts("dve", noml[:], oml[:], -1.0, None, ALU.mult, None, [oml.d], [noml.d])
        P.ts("dve", omlc[:], oml[:], float(128 ** -0.5), None, ALU.mult, None, [oml.d], [omlc.d])
        P.ts("dve", nomlc[:], oml[:], -float(128 ** -0.5), None, ALU.mult, None, [oml.d], [nomlc.d])
        P.barrier()
        P.flush()

        rr_state = {"pmm": 0, "cast": 0}

        def cast_eng():
            rr_state["cast"] += 1
            return ("act", "dve", "pool")[rr_state["cast"] % 3]

        def cast(eng, out, in_, scale, reads, writes):
            if scale is None:
                P.cp(eng, out, in_, reads, writes)
            elif eng == "act":
                P.act(out, in_, AF.Copy, reads, writes, scale=scale)
            else:
                P.ts(eng, out, in_, scale, None, ALU.mult, None, reads, writes)

        for l in range(NL):
            x_src = x_in if l == 0 else xres
            sfx[0] = f"_L{l}"
            with contextlib.ExitStack() as ea:
                Win = sb(ea, "Win", [128, 8, INW], BF16)
                stg = [sb(ea, f"stgA{i}", [128, 1792], F32) for i in range(2)]
                xt = sb(ea, "xt", [128, 4, D], F32)
                xn = sb(ea, "xn", [128, 4, D], BF16, nd=4)
                hT = sb(ea, "hT", [128, 8, TT], BF16, nd=8)
                junk = sb(ea, "junk", [128, D], BF16)
                ss = sb(ea, "ss", [128, 4], F32)
                rt = sb(ea, "rt", [128, 4], F32)
                rs = sb(ea, "rs", [128, 4], F32)
                cs = sb(ea, "cs", [128, TT], F32)
                sn = sb(ea, "sn", [128, TT], F32)
                qsb = [sb(ea, f"qsb{i}", [128, TT], BF16) for i in range(2)]
                t1 = [sb(ea, f"t1_{i}", [128, TT], F32) for i in range(2)]
                t2 = [sb(ea, f"t2_{i}", [128, TT], F32) for i in range(2)]
                qst = sb(ea, "qst", [128, 4, TT], BF16)
                kst = sb(ea, "kst", [128, 4, TT], BF16)
                vst = sb(ea, "vst", [128, 4, 512], BF16)
                hv = sb(ea, "hv", [128, 4, 512], BF16)
                sil = sb(ea, "sil", [128, TT], F32)
                sg = sb(ea, "sg", [128, TT], F32)
                lf = sb(ea, "lf", [128, TT], F32)
                kk = sb(ea, "kk", [128, TT], F32)
                Lc = sb(ea, "Lc", [128, 8, 64], F32)
                Lst = sb(ea, "Lst", [128, 8], F32)
                Dd = [sb(ea, f"Dd{i}", [128, 8, 64], F32) for i in range(2)]
                Ee = [sb(ea, f"Ee{i}", [128, 8, 64], F32) for i in range(2)]
                dtmp = sb(ea, "dtmp", [128, 8], F32)
                dch = [sb(ea, f"dch{h}", [128, 8], F32) for h in range(4)]
                hqm = [sb(ea, f"hqm{h}", [128, TT], BF16) for h in range(4)]
                hqs = [sb(ea, f"hqs{h}", [128, TT], BF16) for h in range(4)]
                hkm = [sb(ea, f"hkm{h}", [128, TT], BF16) for h in range(4)]
                hke = [sb(ea, f"hke{h}", [128, TT], BF16) for h in range(4)]
                gate = [sb(ea, f"gate{h}", [128, TT], BF16) for h in range(4)]
                hketm = [sb(ea, f"hketm{h}", [128, 4, 128], BF16) for h in range(4)]
                am = [sb(ea, f"am{h}", [128, 256], BF16) for h in range(4)]
                osb = sb(ea, "osb", [128, TT], F32)
                sq = sb(ea, "sq", [128, TT], BF16)
                rno = sb(ea, "rno", [128, TT], F32)
                mst = sb(ea, "mst", [128, 4, TT], BF16)

                pi = 0
                for kc in range(8):
                    for hf in range(2):
                        st = stg[pi % 2]
                        load(st, st[:], w_in[l, kc * 128:(kc + 1) * 128, hf * 1792:(hf + 1) * 1792],
                             eng=("sp", "pool")[pi % 2])
                        cast(cast_eng(), Win[:, kc, hf * 1792:(hf + 1) * 1792], st[:], nmix[:, l, kc:kc + 1],
                             [st.d], [Win.d])
                        pi += 1
                for h in range(4):
                    P.memset("pool", Tst[h][:], 0.0, [Tst[h].d])
                    P.memset("pool", Tbf[h][:], 0.0, [Tbf[h].d])
                P.memset("pool", Lst[:, 0:1], 0.0, [Lst.d])

                pmm_i = [0]

                def next_pmm():
                    pmm_i[0] += 1
                    return pb[pmm_i[0] % 2]

                def fm_chunk(col0):
                    b = next_pmm()
                    for kc in range(8):
                        P.mm(b[:, :], Win[:, kc, col0:col0 + 128], hT[:, kc, :], kc == 0, kc == 7,
                             [Win.d, hT.ds[kc]], [b.d])
                    return b

                def tm_sub(col0, sub):
                    b = next_pmm()
                    for kc in range(8):
                        P.mm(b[:, :], hT[:, kc, sub * 128:(sub + 1) * 128], Win[:, kc, col0:col0 + 512], kc == 0,
                             kc == 7, [Win.d, hT.ds[kc]], [b.d])
                    return b

                c_hq = float(128 ** -0.5)
                for tt in range(NT):
                    t0 = tt * TT
                    load(xt, xt[:], x_src[t0:t0 + TT, :].rearrange("(s p) d -> p s d", p=128))
                    load(cs, cs[:], c_cos[:, t0:t0 + TT], eng="sp")
                    load(sn, sn[:], c_sin[:, t0:t0 + TT], eng="sp")
                    P.memset("dve", ss[:], 0.0, [ss.d])
                    for sub in range(4):
                        P.act(junk[:], xt[:, sub, :], AF.Square, [xt.d], [junk.d, ss.d], accum=ss[:, sub:sub + 1])
                    P.act(rt[:], ss[:], AF.Sqrt, [ss.d], [rt.d], scale=1.0 / D, bias=eps_t[:, 0:1])
                    P.op("dve", lambda e: e.reciprocal(out=rs[:], in_=rt[:]), [rt.d], [rs.d])
                    for sub in range(4):
                        eng = ("act", "dve", "dve", "act")[sub]
                        cast(eng, xn[:, sub, :], xt[:, sub, :], rs[:, sub:sub + 1], [xt.d, rs.d], [xn.ds[sub]])
                    for kc in range(8):
                        hf = kc % 2
                        for sub in range(4):
                            P.tr(ptb[:, hf * 512 + sub * 128: hf * 512 + (sub + 1) * 128],
                                 xn[:, sub, kc * 128:(kc + 1) * 128], ident[:], [xn.ds[sub], ident.d], [ptb.ds[hf]])
                        P.cp(("act", "dve")[kc % 2], hT[:, kc, :], ptb[:, hf * 512:(hf + 1) * 512], [ptb.ds[hf]],
                             [hT.ds[kc]])

                    for hh in range(4):
                        b = fm_chunk(2048 + hh * 128)
                        P.act(sg[:], b[:, :], AF.Sigmoid, [b.d], [sg.d])
                        P.act(lf[:], sg[:], AF.Ln, [sg.d], [lf.d], scale=oml[:, l, hh:hh + 1], bias=lb[:, l, hh:hh + 1])
                        P.ts("dve", kk[:], sg[:], nomlc[:, l, hh:hh + 1], omlc[:, l, hh:hh + 1], ALU.mult, ALU.add,
                             [sg.d], [kk.d])
                        Lflat = Lc[:].rearrange("p c i -> p (c i)")
                        P.op("dve", lambda e, Lflat=Lflat: e.tensor_tensor_scan(out=Lflat, data0=ones_f[:], data1=lf[:],
                                                                                 initial=0.0, op0=ALU.mult, op1=ALU.add),
                             [ones_f.d, lf.d], [Lc.d])
                        P.cp("dve", Lst[:, 1:8], Lc[:, 0:7, 63], [Lc.d], [Lst.d])
                        P.tt("dve", dtmp[:], Lc[:, :, 63], Lst[:], ALU.subtract, [Lc.d, Lst.d], [dtmp.d])
                        P.act(dch[hh][:], dtmp[:], AF.Exp, [dtmp.d], [dch[hh].d])
                        b = fm_chunk(1536 + hh * 128)
                        P.act(sil[:], b[:, :], AF.Silu, [b.d], [sil.d])
                        bc = lambda ap: ap.broadcast_to([128, 8, 64])
                        sil3 = sil[:].rearrange("p (c i) -> p c i", i=64)
                        kk3 = kk[:].rearrange("p (c i) -> p c i", i=64)
                        v3 = lambda bf: bf[:].rearrange("p (c i) -> p c i", i=64)
                        D0, D1, E0, E1 = Dd[0], Dd[1], Ee[0], Ee[1]
                        P.tt("dve", D0[:], Lc[:], bc(Lc[:, :, 31:32]), ALU.subtract, [Lc.d], [D0.d])
                        P.act(E0[:], D0[:], AF.Exp, [D0.d], [E0.d])
                        P.act(E1[:], D0[:], AF.Exp, [D0.d], [E1.d], scale=-1.0)
                        P.tt("dve", v3(hqm[hh]), sil3, E0[:], ALU.mult, [sil.d, E0.d], [hqm[hh].d])
                        P.tt("pool", v3(hkm[hh]), kk3, E1[:], ALU.mult, [kk.d, E1.d], [hkm[hh].d])
                        P.tt("dve", D1[:], Lc[:], bc(Lst[:].rearrange("p (c o) -> p c o", o=1)), ALU.subtract,
                             [Lc.d, Lst.d], [D1.d])
                        P.act(E0[:], D1[:], AF.Exp, [D1.d], [E0.d])
                        P.tt("pool", v3(hqs[hh]), sil3, E0[:], ALU.mult, [sil.d, E0.d], [hqs[hh].d])
                        P.tt("dve", D0[:], bc(Lc[:, :, 63:64]), Lc[:], ALU.subtract, [Lc.d], [D0.d])
                        P.act(E1[:], D0[:], AF.Exp, [D0.d], [E1.d])
                        P.tt("dve", v3(hke[hh]), kk3, E1[:], ALU.mult, [kk.d, E1.d], [hke[hh].d])
                        b = fm_chunk(3072 + hh * 128)
                        P.act(gate[hh][:], b[:, :], AF.Silu, [b.d], [gate[hh].d])
                    for sub in range(4):
                        b = tm_sub(2560, sub)
                        P.cp(("act", "dve")[sub % 2], hv[:, sub, :], b[:, :], [b.d], [hv.d])

                    for which, (col0, stt_, scl) in enumerate([(0, qst, 0.125), (512, kst, None)]):
                        for m in range(4):
                            b = fm_chunk(col0 + m * 128)
                            qs_ = qsb[m % 2]
                            if scl is None:
                                P.cp("act", qs_[:], b[:, :], [b.d], [qs_.d])
                            else:
                                P.act(qs_[:], b[:, :], AF.Copy, [b.d], [qs_.d], scale=scl)
                            pr = pb[2]
                            P.mm(pr[:, :], rot[:], qs_[:], True, True, [rot.d, qs_.d], [pr.d])
                            a1, a2 = t1[m % 2], t2[m % 2]
                            P.tt("dve", a2[:], pr[:, :], sn[:], ALU.mult, [pr.d, sn.d], [a2.d])
                            P.tt("pool", a1[:], qs_[:], cs[:], ALU.mult, [qs_.d, cs.d], [a1.d])
                            P.tt("dve", stt_[:, m, :], a1[:], a2[:], ALU.add, [a1.d, a2.d], [stt_.d])
                        dst = (qr_d, kr_d)[which]
                        P.dma("sp", dst[:, :, t0:t0 + TT].rearrange("m p t -> p m t"), stt_[:], [stt_.d], [],
                              dsem(stt_))
                    for sub in range(4):
                        b = tm_sub(1024, sub)
                        P.cp(("act", "dve")[sub % 2], vst[:, sub, :], b[:, :], [b.d], [vst.d])
                    P.dma("sp", va_d[t0:t0 + TT, :].rearrange("(s p) c -> p s c", p=128), vst[:], [vst.d], [],
                          dsem(vst))

                    for hp in range(2):
                        heads = (2 * hp, 2 * hp + 1)
                        pA = pb[3]
                        pSb = {heads[0]: pb[3], heads[1]: pb[4]}
                        pO = {heads[0]: pb[5], heads[1]: pb[6]}
                        for hh in heads:
                            hf = hh % 2
                            for sub in range(4):
                                P.tr(ptb[:, hf * 512 + sub * 128: hf * 512 + (sub + 1) * 128],
                                     hke[hh][:, sub * 128:(sub + 1) * 128], ident[:], [hke[hh].d, ident.d],
                                     [ptb.ds[hf]])
                            P.cp("act", hketm[hh][:].rearrange("p s f -> p (s f)"), ptb[:, hf * 512:(hf + 1) * 512],
                                 [ptb.ds[hf]], [hketm[hh].d])
                            for sub in range(4):
                                for e_ in range(2):
                                    tk = sub * 128 + e_ * 64
                                    P.mm(pA[e_ * 64:(e_ + 1) * 64, hf * 256 + sub * 64: hf * 256 + (sub + 1) * 64],
                                         hkm[hh][:, tk:tk + 64], hqm[hh][:, tk:tk + 64], True, True,
                                         [hkm[hh].d, hqm[hh].d], [pA.d])
                            P.tt("dve", am[hh][:], pA[:, hf * 256:(hf + 1) * 256], hmask[:], ALU.mult,
                                 [pA.d, hmask.d], [am[hh].d])
                        for c in range(8):
                            sub, e_ = c // 2, c % 2
                            tk = sub * 128 + e_ * 64
                            rows = slice(e_ * 64, (e_ + 1) * 64)
                            for hh in heads:
                                slot = c % 4
                                pS = pSb[hh]
                                pSs = pS[:, slot * 128:(slot + 1) * 128]
                                vch = hv[rows, sub, hh * 128:(hh + 1) * 128]
                                po = pO[hh]
                                P.mm(po[:, c * 64:(c + 1) * 64], Tbf[hh][:], hqs[hh][:, tk:tk + 64], True, False,
                                     [Tbf[hh].d, hqs[hh].d], [po.d])
                                P.mm(po[:, c * 64:(c + 1) * 64], vch, am[hh][rows, sub * 64:(sub + 1) * 64], False, True,
                                     [hv.d, am[hh].d], [po.d])
                                P.mm(pSs, hketm[hh][rows, sub, :], vch, True, True, [hketm[hh].d, hv.d], [pS.d])
                                P.stt("dve", Tst[hh][:], Tst[hh][:], dch[hh][:, c:c + 1], pSs, ALU.mult, ALU.add,
                                      [pS.d, dch[hh].d], [Tst[hh].d])
                                P.cp("dve", Tbf[hh][:], Tst[hh][:], [Tst[hh].d], [Tbf[hh].d])
                        for hh in heads:
                            po = pO[hh]
                            P.cp("dve", osb[:], po[:, :], [po.d], [osb.d])
                            P.act(sq[:], po[:, :], AF.Square, [po.d], [sq.d])
                            pn = pb[2]
                            P.mm(pn[:, :], onesm[:], sq[:], True, True, [onesm.d, sq.d], [pn.d])
                            P.act(rno[:], pn[:, :], AF.Sqrt, [pn.d], [rno.d], bias=eps_t[:, 0:1])
                            P.op("dve", lambda e: e.reciprocal(out=rno[:], in_=rno[:]), [rno.d], [rno.d])
                            P.tt("dve", osb[:], osb[:], rno[:], ALU.mult, [rno.d], [osb.d])
                            P.tt("pool", mst[:, hh, :], osb[:], gate[hh][:], ALU.mult, [osb.d, gate[hh].d], [mst.d])
                    P.dma("sp", mix_d[4:8, :, t0:t0 + TT].rearrange("m p t -> p m t"), mst[:], [mst.d], [], dsem(mst))
                P.barrier()
                P.flush()
                P.recycle()

            with contextlib.ExitStack() as eb:
                q2 = sb(eb, "q2", [128, S], BF16)
                k2 = sb(eb, "k2", [128, S], BF16)
                qp = [q2] + [sb(eb, f"qp{i}", [128, S], BF16) for i in (1, 2)]
                kp = [k2] + [sb(eb, f"kp{i}", [128, S], BF16) for i in (1, 2)]
                vt = [[sb(eb, f"vt{j}_{i}", [128, 32, 128], BF16) for i in range(3)] for j in range(2)]
                acc = [sb(eb, f"acc{j}", [128, S], F32) for j in range(2)]
                PT = [sb(eb, f"PT{i}", [128, 256], BF16) for i in range(3)]
                lsh = sb(eb, "lsh", [64, S], F32)
                ost = [sb(eb, f"ost{j}", [64, S], BF16) for j in range(2)]
                wst = [sb(eb, f"wst{i}", [128, 1024], F32) for i in range(3)]
                wsb = [sb(eb, f"wsb{i}", [128, 1024], BF16) for i in range(3)]

                def prep_gen():
                    i = 0
                    for kc in range(8):
                        for q in range(4):
                            a, b_ = wst[i % 3], wsb[i % 3]
                            load(a, a[:], w_up[l, kc * 128:(kc + 1) * 128, q * 1024:(q + 1) * 1024], eng="pool")
                            cast("pool", b_[:], a[:], nmlp[:, l, kc:kc + 1], [a.d], [b_.d])
                            P.dma("pool", wup_d[:, kc, q * 1024:(q + 1) * 1024], b_[:], [b_.d], [], dsem(b_, "pool"))
                            i += 1
                            yield
                    for hc in range(32):
                        a, b_ = wst[i % 3], wsb[i % 3]
                        load(a, a[:], w_down[l, hc * 128:(hc + 1) * 128, :], eng="pool")
                        cast("pool", b_[:], a[:], None, [a.d], [b_.d])
                        P.dma("pool", wdn_d[:, hc, :], b_[:], [b_.d], [], dsem(b_, "pool"))
                        i += 1
                        yield

                pg = prep_gen()
                for j in range(2):
                    for i in range(3):
                        P.memset("pool", vt[j][i][:, :, 64:128], 1.0, [vt[j][i].d])
                def load_v(h_):
                    for pi_, dl in enumerate(PATTERNS):
                        v_ = vt[h_ % 2][pi_]
                        vsrc = va_d[:, h_ * 64:(h_ + 1) * 64].rearrange("(n j dl) c -> dl j n c", j=128, dl=dl)
                        nblk = 32 // dl
                        for r in range(dl):
                            P.dma("sp", v_[:, r * nblk:(r + 1) * nblk, 0:64], vsrc[r], [], [v_.d], dsem(v_))

                psr = [pb[0], pb[1], pb[2]]
                por = [pb[3], pb[4], pb[5]]
                blk_i = 0
                for m in range(4):
                    load(q2, q2[:], qr_d[m])
                    load(k2, k2[:], kr_d[m], eng="sp")
                    for pi_, dl in enumerate(PATTERNS):
                        if dl == 1:
                            continue
                        P.cp("dve", qp[pi_][:], q2[:].rearrange("p (m r) -> p r m", r=dl), [q2.d], [qp[pi_].d])
                        P.cp("act", kp[pi_][:], k2[:].rearrange("p (m r) -> p r m", r=dl), [k2.d], [kp[pi_].d])
                    for e_ in range(2):
                        h = 2 * m + e_
                        rows = slice(e_ * 64, (e_ + 1) * 64)
                        ac = acc[e_]
                        if h == 0:
                            load_v(0)
                        if h + 1 < 8:
                            load_v(h + 1)
                        blocks = []
                        for pi_, dl in enumerate(PATTERNS):
                            nblk = 32 // dl
                            acv = ac[:].rearrange("p (m r) -> p r m", r=dl)
                            for r in range(dl):
                                for n in range(nblk):
                                    blocks.append((pi_, dl, nblk, r, n, acv))

                        def emit_scores(i):
                            pi_, dl, nblk, r, n, acv = blocks[i]
                            qq, kk_ = qp[pi_], kp[pi_]
                            bq = r * nblk + n
                            nq = 256 if n < nblk - 1 else 128
                            pS_ = psr[i % 3]
                            pt_ = PT[i % 3]
                            P.mm(pS_[:, 0:nq], kk_[rows, bq * 128:(bq + 1) * 128],
                                 qq[rows, bq * 128:bq * 128 + nq], True, False, [kk_.d, qq.d], [pS_.d])
                            P.mm(pS_[:, 0:nq], ident[:], mbias[:, 0:nq], False, True, [ident.d, mbias.d],
                                 [pS_.d])
                            P.act(pt_[:, 0:nq], pS_[:, 0:nq], AF.Exp, [pS_.d], [pt_.d])

                        def emit_pv(i):
                            pi_, dl, nblk, r, n, acv = blocks[i]
                            v_ = vt[e_][pi_]
                            bq = r * nblk + n
                            pt_ = PT[i % 3]
                            g, sl = divmod(bq, 4)
                            po = por[g % 3]
                            P.mm(po[:, sl * 128:(sl + 1) * 128], v_[:, bq, :], pt_[:, 0:128], n == 0, True,
                                 [v_.d, pt_.d], [po.d])
                            if n < nblk - 1:
                                g2, sl2 = divmod(bq + 1, 4)
                                po2 = por[g2 % 3]
                                P.mm(po2[:, sl2 * 128:(sl2 + 1) * 128], v_[:, bq, :], pt_[:, 128:256], True,
                                     False, [v_.d, pt_.d], [po2.d])
                            if (sl == 3) or (n == nblk - 1):
                                first = max(g * 4, r * nblk)
                                cnt = (bq - first + 1) * 128
                                n_first = first - r * nblk
                                dstv = acv[:, r, n_first * 128: n_first * 128 + cnt]
                                srcv = po[:, (first - g * 4) * 128:(first - g * 4) * 128 + cnt]
                                if pi_ == 0:
                                    P.cp("dve", dstv, srcv, [po.d], [ac.d])
                                else:
                                    P.tt("dve", dstv, dstv, srcv, ALU.add, [po.d], [ac.d])

                        LOOK = 2
                        for i in range(len(blocks) + LOOK):
                            if i < len(blocks):
                                emit_scores(i)
                            if i >= LOOK:
                                emit_pv(i - LOOK)
                            blk_i += 1
                            if blk_i % 6 == 0:
                                next(pg, None)
                        P.cp("dve", lsh[:], ac[64:128, :], [ac.d], [lsh.d])
                        P.op("dve", lambda e: e.reciprocal(out=lsh[:], in_=lsh[:]), [lsh.d], [lsh.d])
                        P.tt("dve", ost[e_][:], ac[0:64, :], lsh[:], ALU.mult, [ac.d, lsh.d], [ost[e_].d])
                        P.dma("sp", mix_d[m, e_ * 64:(e_ + 1) * 64, :], ost[e_][:], [ost[e_].d], [], dsem(ost[e_]))
                for _ in pg:
                    pass
                P.barrier()
                P.flush()
                P.recycle()

            with contextlib.ExitStack() as ec:
                Wo = sb(ec, "Wo", [128, 8, D], BF16)
                stg = [sb(ec, f"stgC{i}", [128, 1024], F32) for i in range(2)]
                xt = sb(ec, "xtC", [128, 4, D], F32)
                mt = sb(ec, "mt", [128, 8, TT], BF16)
                sqa = sb(ec, "sqa", [128, 4, TT], BF16)
                rsa = sb(ec, "rsa", [128, 4], F32)
                tmp = [sb(ec, f"tmpC{i}", [128, 512], F32) for i in range(2)]
                xn = sb(ec, "xnC", [128, 4, D], BF16, nd=4)
                hT = sb(ec, "hTC", [128, 8, TT], BF16, nd=8)
                junk = sb(ec, "junkC", [128, D], BF16)
                ss = sb(ec, "ssC", [128, 4], F32)
                rt = sb(ec, "rtC", [128, 4], F32)
                rs = sb(ec, "rsC", [128, 4], F32)
                uT = sb(ec, "uT", [128, 32, TT], BF16, nd=32)
                rl = [sb(ec, f"rl{i}", [128, TT], BF16) for i in range(2)]
                wu = [sb(ec, f"wu{i}", [128, 8, 512], BF16) for i in range(3)]
                wd = [sb(ec, f"wd{i}", [128, 4, 512], BF16) for i in range(4)]
                xo = sb(ec, "xo", [128, 4, D], F32)
                nfin = sb(ec, "nfin", [128, D], F32)
                last = (l == NL - 1)
                if last:
                    load(nfin, nfin[:], p_nfin)
                for kc in range(8):
                    st = stg[kc % 2]
                    load(st, st[:], w_out[l, kc * 128:(kc + 1) * 128, :], eng=("sp", "pool")[kc % 2])
                    gsc = again[:, l, kc:kc + 1] if kc < 4 else hgain[:, l, 0:1]
                    cast(cast_eng(), Wo[:, kc, :], st[:], gsc, [st.d], [Wo.d])
                gen_i = [0]

                def gbank():
                    gen_i[0] += 1
                    return pb[gen_i[0] % 3]

                for tt in range(NT):
                    t0 = tt * TT
                    load(xt, xt[:], x_src[t0:t0 + TT, :].rearrange("(s p) d -> p s d", p=128))
                    load(mt, mt[:], mix_d[:, :, t0:t0 + TT].rearrange("m p t -> p m t"), eng="sp")
                    P.act(sqa[:], mt[:, 0:4, :], AF.Square, [mt.d], [sqa.d])
                    pss = pb[6]
                    for sub in range(4):
                        for kc in range(4):
                            P.mm(pss[:, sub:sub + 1], sqa[:, kc, sub * 128:(sub + 1) * 128], onec[:], kc == 0, kc == 3,
                                 [sqa.d, onec.d], [pss.d])
                    P.act(rsa[:], pss[:, 0:4], AF.Sqrt, [pss.d], [rsa.d], scale=1.0 / 512.0, bias=eps_t[:, 0:1])
                    P.op("dve", lambda e: e.reciprocal(out=rsa[:], in_=rsa[:]), [rsa.d], [rsa.d])
                    for sub in range(4):
                        for nh in range(2):
                            pa = gbank()
                            for kc in range(4):
                                P.mm(pa[:, :], mt[:, kc, sub * 128:(sub + 1) * 128], Wo[:, kc, nh * 512:(nh + 1) * 512],
                                     kc == 0, kc == 3, [mt.d, Wo.d], [pa.d])
                            ph = gbank()
                            for kc in range(4, 8):
                                P.mm(ph[:, :], mt[:, kc, sub * 128:(sub + 1) * 128], Wo[:, kc, nh * 512:(nh + 1) * 512],
                                     kc == 4, kc == 7, [mt.d, Wo.d], [ph.d])
                            xs_ = xt[:, sub, nh * 512:(nh + 1) * 512]
                            tm_ = tmp[nh]
                            P.stt("dve", tm_[:], pa[:, :], rsa[:, sub:sub + 1], xs_, ALU.mult, ALU.add,
                                  [pa.d, rsa.d, xt.d], [tm_.d])
                            P.tt("dve", xs_, tm_[:], ph[:, :], ALU.add, [tm_.d, ph.d], [xt.d])
                    P.memset("dve", ss[:], 0.0, [ss.d])
                    for sub in range(4):
                        P.act(junk[:], xt[:, sub, :], AF.Square, [xt.d], [junk.d, ss.d], accum=ss[:, sub:sub + 1])
                    P.act(rt[:], ss[:], AF.Sqrt, [ss.d], [rt.d], scale=1.0 / D, bias=eps_t[:, 0:1])
                    P.op("dve", lambda e: e.reciprocal(out=rs[:], in_=rt[:]), [rt.d], [rs.d])
                    for sub in range(4):
                        eng = ("act", "dve", "dve", "act")[sub]
                        cast(eng, xn[:, sub, :], xt[:, sub, :], rs[:, sub:sub + 1], [xt.d, rs.d], [xn.ds[sub]])
                    for kc in range(8):
                        hf = kc % 2
                        for sub in range(4):
                            P.tr(ptb[:, hf * 512 + sub * 128: hf * 512 + (sub + 1) * 128],
                                 xn[:, sub, kc * 128:(kc + 1) * 128], ident[:], [xn.ds[sub], ident.d], [ptb.ds[hf]])
                        P.cp(("act", "dve")[kc % 2], hT[:, kc, :], ptb[:, hf * 512:(hf + 1) * 512], [ptb.ds[hf]],
                             [hT.ds[kc]])
                    for g in range(8):
                        w_ = wu[(tt * 8 + g) % 3]
                        load(w_, w_[:], wup_d[:, :, g * 512:(g + 1) * 512], eng="sp")
                        for hl in range(4):
                            hc = g * 4 + hl
                            pu = gbank()
                            for kc in range(8):
                                P.mm(pu[:, :], w_[:, kc, hl * 128:(hl + 1) * 128], hT[:, kc, :], kc == 0, kc == 7,
                                     [w_.d, hT.ds[kc]], [pu.d])
                            r_ = rl[hc % 2]
                            P.act(r_[:], pu[:, :], AF.Relu, [pu.d], [r_.d])
                            P.tt(("pool", "dve")[hc % 2], uT[:, hc, :], r_[:], r_[:], ALU.mult, [r_.d], [uT.ds[hc]])
                    for nh in range(2):
                        py = [pb[3], pb[4], pb[5], pb[6]]
                        for g in range(8):
                            wi_ = (nh * 8 + g) % 4
                            w_ = wd[wi_]
                            load(w_, w_[:], wdn_d[:, g * 4:(g + 1) * 4, nh * 512:(nh + 1) * 512], eng="sp")
                            for sub in range(4):
                                for hl in range(4):
                                    hc = g * 4 + hl
                                    P.mm(py[sub][:, :], uT[:, hc, sub * 128:(sub + 1) * 128], w_[:, hl, :], hc == 0,
                                         hc == 31, [uT.ds[hc], w_.d], [py[sub].d])
                        for sub in range(4):
                            P.tt("dve", xo[:, sub, nh * 512:(nh + 1) * 512], xt[:, sub, nh * 512:(nh + 1) * 512],
                                 py[sub][:, :], ALU.add, [xt.d, py[sub].d], [xo.d])
                    if not last:
                        P.dma("sp", xres[t0:t0 + TT, :].rearrange("(s p) d -> p s d", p=128), xo[:], [xo.d], [],
                              dsem(xo))
                    else:
                        P.memset("dve", ss[:], 0.0, [ss.d])
                        for sub in range(4):
                            P.act(junk[:], xo[:, sub, :], AF.Square, [xo.d], [junk.d, ss.d], accum=ss[:, sub:sub + 1])
                        P.act(rt[:], ss[:], AF.Sqrt, [ss.d], [rt.d], scale=1.0 / D, bias=eps_t[:, 0:1])
                        P.op("dve", lambda e: e.reciprocal(out=rs[:], in_=rt[:]), [rt.d], [rs.d])
                        for sub in range(4):
                            P.stt("dve", xo[:, sub, :], xo[:, sub, :], rs[:, sub:sub + 1], nfin[:],
                                  ALU.mult, ALU.mult, [rs.d, nfin.d], [xo.d])
                        P.dma("sp", y_out[t0:t0 + TT, :].rearrange("(s p) d -> p s d", p=128), xo[:], [xo.d], [],
                              dsem(xo))
                P.barrier()
                P.flush()
                P.recycle()
    return nc


def _consts():
    f32 = np.float32
    half = 32
    inv = (10000.0 ** (-np.arange(half, dtype=f32) / half)).astype(f32)
    pos = np.arange(S, dtype=f32)
    ang = (pos[:, None] * inv[None, :]).astype(f32)
    cos, sin = np.cos(ang).astype(f32), np.sin(ang).astype(f32)
    fidx = (np.arange(128) % 64) % 32
    c_cos = np.ascontiguousarray(cos[:, fidx].T)
    c_sin = np.ascontiguousarray(sin[:, fidx].T)
    rot = np.zeros((128, 128), f32)
    for m in range(128):
        if (m % 64) < 32:
            rot[m + 32, m] = -1.0
        else:
            rot[m - 32, m] = 1.0
    ident = np.eye(128, dtype=f32)
    j = np.arange(128)[:, None]
    i = np.arange(128)[None, :]
    mb = np.concatenate([np.where(j <= i, 0.0, NEG), np.where(j >= i, 0.0, NEG)], axis=1).astype(f32)
    jj = (np.arange(128) % 64)[:, None]
    ii = np.arange(64)[None, :]
    hm = np.tile((jj <= ii).astype(f32), (1, 4))
    return dict(c_cos=c_cos, c_sin=c_sin, c_rot=rot, c_ident=ident, c_mbias=mb, c_hmask=hm)


_CACHE = {}


def kernel(x, norm_mix, w_in, attn_out_gain, hgrn_lb_logits, hgrn_out_gain, w_out, norm_mlp, w_up, w_down,
           norm_final):
    f32 = np.float32
    a = lambda v: np.ascontiguousarray(np.asarray(v, dtype=f32))
    if "nc" not in _CACHE:
        _CACHE["nc"] = build()
    nc = _CACHE["nc"]
    col = lambda v, k: np.ascontiguousarray(a(v).reshape(NL, k, 128).transpose(2, 0, 1))
    shared = dict(
        w_in=a(w_in), w_out=a(w_out), w_up=a(w_up), w_down=a(w_down),
        p_nmix=col(norm_mix, 8), p_again=col(attn_out_gain, 4), p_lbl=col(hgrn_lb_logits, 4),
        p_hgain=col(hgrn_out_gain, 1), p_nmlp=col(norm_mlp, 8),
        p_nfin=np.ascontiguousarray(np.broadcast_to(a(norm_final)[None, :], (128, D))),
    )
    shared.update(_consts())
    xs = a(x)
    in_maps = [dict(shared, x=np.ascontiguousarray(xs[b])) for b in range(8)]
    res = run_bass_kernel_spmd(nc, in_maps, core_ids=list(range(8)))
    _CACHE["res"] = res
    return np.stack([r["y"] for r in res.results], axis=0).astype(f32)
```
